# Optimizing a Trainium2 kernel written in Bass

```python
import math
import jax, jax.numpy as jnp
from jax import lax
import numpy as np

D_MODEL = 1024
BATCH = 2
SEQ = 8192
DEPTH = 2

HEAD_DIM = 64
MIX_WIDTH = D_MODEL
N_MIXERS = 4
HEADS_PER_MIXER = MIX_WIDTH // (N_MIXERS * HEAD_DIM)
GROUP_WIDTH = HEADS_PER_MIXER * HEAD_DIM
N_HEADS_TOTAL = N_MIXERS * HEADS_PER_MIXER
QUERY_BLOCK = 128
NUM_BUCKETS = 32
REL_MAX_DIST = 2048
DILATED_CONFIGS = ((128, 1), (512, 4), (2048, 16))
SWA_WINDOW = 128
SWA_KV_HEADS = HEADS_PER_MIXER // 2
DIFF_QK_DIM = HEAD_DIM // 2
CMP_LEN = 32
CMP_STRIDE = 16
CMP_HIDDEN = 256
SLC_BLOCK = 64
SLC_TOPK = 16
NSA_WINDOW = 512
NSA_BRANCHES = 3
RMS_EPS = 1e-6
NEG_INF = -1e30
FORCE_SELECT = 1e9
TINY = 1e-30
PROJ_SIZES = (
    GROUP_WIDTH, GROUP_WIDTH, GROUP_WIDTH,
    GROUP_WIDTH, SWA_KV_HEADS * HEAD_DIM, SWA_KV_HEADS * HEAD_DIM,
    GROUP_WIDTH, GROUP_WIDTH, GROUP_WIDTH,
    GROUP_WIDTH, HEAD_DIM, HEAD_DIM, HEAD_DIM, HEAD_DIM, HEAD_DIM, HEAD_DIM,
    HEADS_PER_MIXER * NSA_BRANCHES,
    MIX_WIDTH,
)
PROJ_WIDTH = sum(PROJ_SIZES)

kernel_name = 'hybrid_parallel_heads_dilated_swa_diff_nsa'


def rms_norm(t, g):
    tf = t.astype(jnp.float32)
    y = tf * lax.rsqrt(jnp.mean(tf * tf, axis=-1, keepdims=True) + RMS_EPS) * g.astype(jnp.float32)
    return y.astype(t.dtype)


def t5_bucket(dist):
    n = jnp.maximum(dist, 0)
    max_exact = NUM_BUCKETS // 2
    nf = jnp.maximum(n, 1).astype(jnp.float32)
    large = max_exact + (jnp.log(nf / max_exact) / math.log(REL_MAX_DIST / max_exact)
                         * (NUM_BUCKETS - max_exact)).astype(jnp.int32)
    large = jnp.minimum(large, NUM_BUCKETS - 1)
    return jnp.where(n < max_exact, n, large)


def rel_bias(dist, table_h):
    return jnp.take(table_h, t5_bucket(dist), axis=0).astype(jnp.float32)


def masked_probs(s, valid, sink=None):
    s = jnp.where(valid, s, NEG_INF)
    m = jnp.max(s, axis=-1, keepdims=True)
    if sink is not None:
        m = jnp.maximum(m, sink)
    p = jnp.exp(s - m) * valid
    den = jnp.sum(p, axis=-1, keepdims=True)
    if sink is not None:
        den = den + jnp.exp(sink - m)
    return p, m, den


def banded_attention(q, k, v, table_h, max_dist, dist_scale=1, sink=None):
    bt, nh, seq_len, dh = q.shape
    n_prev = -(-max_dist // QUERY_BLOCK)
    nb = -(-seq_len // QUERY_BLOCK)
    pad = nb * QUERY_BLOCK - seq_len
    pad_seq = lambda t: jnp.pad(t, ((0, 0), (0, 0), (0, pad), (0, 0))).reshape(bt, nh, nb, QUERY_BLOCK, dh)
    qb = pad_seq(q)
    front = ((0, 0), (0, 0), (n_prev, 0), (0, 0), (0, 0))
    kbp = jnp.pad(pad_seq(k), front)
    vbp = jnp.pad(pad_seq(v), front)
    kk = jnp.concatenate([kbp[:, :, i:i + nb] for i in range(n_prev + 1)], axis=3)
    vv = jnp.concatenate([vbp[:, :, i:i + nb] for i in range(n_prev + 1)], axis=3)
    s = jnp.einsum('bhnqd,bhnkd->bhnqk', qb, kk).astype(jnp.float32) * (dh ** -0.5)
    qi = jnp.arange(QUERY_BLOCK)
    kj = jnp.arange((n_prev + 1) * QUERY_BLOCK) - n_prev * QUERY_BLOCK
    dist = qi[:, None] - kj[None, :]
    kpos = jnp.arange(nb)[:, None] * QUERY_BLOCK + kj[None, :]
    valid = ((dist >= 0) & (dist <= max_dist))[None] & (kpos >= 0)[:, None, :]
    s = s + rel_bias(dist * dist_scale, table_h).transpose(2, 0, 1)[:, None]
    if sink is not None:
        sink = sink.astype(jnp.float32).reshape(-1, 1, 1, 1)
    p, m, den = masked_probs(s, valid, sink)
    o = jnp.einsum('bhnqk,bhnkd->bhnqd', p, vv.astype(jnp.float32)) / den
    lse = (m + jnp.log(den))[..., 0]
    o = o.reshape(bt, nh, nb * QUERY_BLOCK, dh)[:, :, :seq_len]
    lse = lse.reshape(bt, nh, nb * QUERY_BLOCK)[:, :, :seq_len]
    return o.astype(q.dtype), lse


def dilated_attention(q, k, v, table_h):
    b, nh, s, dh = q.shape
    outs, lses = [], []
    for window, rate in DILATED_CONFIGS:
        fold = lambda t: t.reshape(b, nh, s // rate, rate, dh).transpose(0, 3, 1, 2, 4).reshape(b * rate, nh, s // rate, dh)
        o, lse = banded_attention(fold(q), fold(k), fold(v), table_h, window // rate, dist_scale=rate)
        outs.append(o.reshape(b, rate, nh, s // rate, dh).transpose(0, 2, 3, 1, 4).reshape(b, nh, s, dh).astype(jnp.float32))
        lses.append(lse.reshape(b, rate, nh, s // rate).transpose(0, 2, 3, 1).reshape(b, nh, s))
    wts = jax.nn.softmax(jnp.stack(lses, axis=0), axis=0)
    return jnp.sum(wts[..., None] * jnp.stack(outs, axis=0), axis=0).astype(q.dtype)


def diff_attention(q, k, v, lam, table_h):
    b, nh, s, _, dq = q.shape
    dv = v.shape[-1]
    kpos = jnp.arange(s)
    vf = v.astype(jnp.float32)
    lam = lam.astype(jnp.float32)

    def block(n):
        qb = lax.dynamic_slice_in_dim(q, n * QUERY_BLOCK, QUERY_BLOCK, axis=2)
        sc = jnp.einsum('bhqcd,bhkcd->bhcqk', qb, k).astype(jnp.float32) * (dq ** -0.5)
        qpos = n * QUERY_BLOCK + jnp.arange(QUERY_BLOCK)
        dist = qpos[:, None] - kpos[None, :]
        sc = sc + rel_bias(dist, table_h).transpose(2, 0, 1)[:, None]
        p, _, den = masked_probs(sc, dist >= 0)
        p = p / den
        a = p[:, :, 0] - lam * p[:, :, 1]
        return jnp.einsum('bhqk,bhkd->bhqd', a, vf)

    o = lax.map(block, jnp.arange(s // QUERY_BLOCK))
    return o.transpose(1, 2, 0, 3, 4).reshape(b, nh, s, dv)


def compress(t, pos, w1, b1, w2, b2):
    b, s, dh = t.shape
    chunks = t.reshape(b, s // CMP_STRIDE, CMP_STRIDE, dh)
    blocks = jnp.concatenate([chunks[:, :-1], chunks[:, 1:]], axis=2) + pos
    blocks = blocks.reshape(b, blocks.shape[1], CMP_LEN * dh)
    return jax.nn.gelu(blocks @ w1 + b1) @ w2 + b2


def nsa_attention(q, k_c, v_c, k_s, v_s, k_w, v_w, gates, cmp_pos, cmp_w1, cmp_b1, cmp_w2, cmp_b2, k_gains, table_h):
    b, nh, s, dh = q.shape
    scale = dh ** -0.5
    kc = rms_norm(compress(k_c, cmp_pos[0], cmp_w1[0], cmp_b1[0], cmp_w2[0], cmp_b2[0]), k_gains[0])
    vc = compress(v_c, cmp_pos[1], cmp_w1[1], cmp_b1[1], cmp_w2[1], cmp_b2[1]).astype(jnp.float32)
    n_cmp = kc.shape[1]
    cmp_end = jnp.arange(n_cmp) * CMP_STRIDE + CMP_LEN - 1
    n_blk = s // SLC_BLOCK
    n_sel = min(SLC_TOPK, n_blk)
    blk_ids = jnp.arange(n_blk)
    overlap = ((cmp_end[:, None] - CMP_LEN + 1 < (blk_ids[None, :] + 1) * SLC_BLOCK)
               & (cmp_end[:, None] >= blk_ids[None, :] * SLC_BLOCK)).astype(jnp.float32)
    ks_blocks = rms_norm(k_s, k_gains[1]).reshape(b, n_blk, SLC_BLOCK, dh)
    vs_blocks = v_s.reshape(b, n_blk, SLC_BLOCK, dh)
    b_idx = jnp.arange(b)[:, None, None]

    def chunk(n):
        qb = lax.dynamic_slice_in_dim(q, n * QUERY_BLOCK, QUERY_BLOCK, axis=2)
        qpos = n * QUERY_BLOCK + jnp.arange(QUERY_BLOCK)
        cdist = qpos[:, None] - cmp_end[None, :]
        sc = jnp.einsum('bhqd,bcd->bhqc', qb, kc).astype(jnp.float32) * scale
        sc = sc + rel_bias(cdist, table_h).transpose(2, 0, 1)
        p, _, den = masked_probs(sc, cdist >= 0)
        p = p / jnp.maximum(den, TINY)
        o_cmp = jnp.einsum('bhqc,bcd->bhqd', p, vc)
        imp = jnp.einsum('bhqc,cj->bqj', p, overlap)
        cur = qpos // SLC_BLOCK
        forced = (blk_ids == 0) | (blk_ids == cur[:, None]) | (blk_ids == cur[:, None] - 1)
        imp = jnp.where(forced, FORCE_SELECT, jnp.where(blk_ids <= cur[:, None], imp, NEG_INF))
        _, idx = lax.top_k(imp, n_sel)
        ks = ks_blocks[b_idx, idx].reshape(b, QUERY_BLOCK, n_sel * SLC_BLOCK, dh)
        vs = vs_blocks[b_idx, idx].reshape(b, QUERY_BLOCK, n_sel * SLC_BLOCK, dh)
        kpos = (idx[..., None] * SLC_BLOCK + jnp.arange(SLC_BLOCK)).reshape(b, QUERY_BLOCK, n_sel * SLC_BLOCK)
        sdist = qpos[None, :, None] - kpos
        ss = jnp.einsum('bhqd,bqkd->bhqk', qb, ks).astype(jnp.float32) * scale
        ss = ss + jnp.moveaxis(rel_bias(sdist, table_h), -1, 1)
        p2, _, den2 = masked_probs(ss, (sdist >= 0)[:, None])
        o_slc = jnp.einsum('bhqk,bqkd->bhqd', p2, vs.astype(jnp.float32)) / den2
        return o_cmp, o_slc

    o_cmp, o_slc = lax.map(chunk, jnp.arange(s // QUERY_BLOCK))
    unblock = lambda o: o.transpose(1, 2, 0, 3, 4).reshape(b, nh, s, dh)
    kw = jnp.repeat(rms_norm(k_w, k_gains[2])[:, None], nh, axis=1)
    vw = jnp.repeat(v_w[:, None], nh, axis=1)
    o_win, _ = banded_attention(q, kw, vw, table_h, NSA_WINDOW - 1)
    g = jax.nn.sigmoid(gates.astype(jnp.float32))
    o = g[..., 0:1] * unblock(o_cmp) + g[..., 1:2] * unblock(o_slc) + g[..., 2:3] * o_win.astype(jnp.float32)
    return o.astype(q.dtype)


def setup_inputs(seed: int = 0) -> dict:
    key = jax.random.key(seed)
    ks = jax.random.split(key, 15)
    nrm = lambda k, shape, sc: sc * jax.random.normal(k, shape, jnp.float32)
    return {
        'x': nrm(ks[0], (BATCH, SEQ, D_MODEL), 1.0),
        'rel_bias_table': nrm(ks[1], (NUM_BUCKETS, N_HEADS_TOTAL), 0.5),
        'norm_w': 1.0 + nrm(ks[2], (DEPTH, D_MODEL), 0.02),
        'w_in': nrm(ks[3], (DEPTH, D_MODEL, PROJ_WIDTH), D_MODEL ** -0.5),
        'w_out': nrm(ks[4], (DEPTH, MIX_WIDTH, D_MODEL), MIX_WIDTH ** -0.5),
        'qk_gain': 1.0 + nrm(ks[5], (DEPTH, 8, HEAD_DIM), 0.02),
        'qk_gain_diff': 1.0 + nrm(ks[6], (DEPTH, 2, DIFF_QK_DIM), 0.02),
        'attn_sinks': nrm(ks[7], (DEPTH, HEADS_PER_MIXER), 1.0),
        'diff_lambda': nrm(ks[8], (DEPTH, 4, DIFF_QK_DIM), 0.1),
        'diff_subln': 1.0 + nrm(ks[9], (DEPTH, HEAD_DIM), 0.02),
        'cmp_pos': nrm(ks[10], (DEPTH, 2, CMP_LEN, HEAD_DIM), 0.02),
        'cmp_w1': nrm(ks[11], (DEPTH, 2, CMP_LEN * HEAD_DIM, CMP_HIDDEN), (CMP_LEN * HEAD_DIM) ** -0.5),
        'cmp_b1': nrm(ks[12], (DEPTH, 2, CMP_HIDDEN), 0.02),
        'cmp_w2': nrm(ks[13], (DEPTH, 2, CMP_HIDDEN, HEAD_DIM), CMP_HIDDEN ** -0.5),
        'cmp_b2': nrm(ks[14], (DEPTH, 2, HEAD_DIM), 0.02),
    }


def reference(x, rel_bias_table, norm_w, w_in, w_out, qk_gain, qk_gain_diff, attn_sinks, diff_lambda,
              diff_subln, cmp_pos, cmp_w1, cmp_b1, cmp_w2, cmp_b2):
    b, s, _ = x.shape
    split_points = np.cumsum(PROJ_SIZES)[:-1]
    tables = [rel_bias_table[:, m * HEADS_PER_MIXER:(m + 1) * HEADS_PER_MIXER] for m in range(N_MIXERS)]
    heads = lambda t: t.reshape(b, s, -1, HEAD_DIM).transpose(0, 2, 1, 3)
    merge = lambda o: o.transpose(0, 2, 1, 3).reshape(b, s, -1).astype(x.dtype)
    rep = HEADS_PER_MIXER // SWA_KV_HEADS
    for layer in range(DEPTH):
        g = qk_gain[layer]
        h = rms_norm(x, norm_w[layer]) @ w_in[layer]
        (a_q, a_k, a_v, b_q, b_k, b_v, c_q, c_k, c_v, d_q, d_kc, d_vc, d_ks, d_vs, d_kw, d_vw,
         d_gate, silu_gate) = jnp.split(h, split_points, axis=-1)
        o_a = dilated_attention(rms_norm(heads(a_q), g[0]), rms_norm(heads(a_k), g[1]), heads(a_v), tables[0])
        kb = jnp.repeat(rms_norm(heads(b_k), g[3]), rep, axis=1)
        vb = jnp.repeat(heads(b_v), rep, axis=1)
        o_b, _ = banded_attention(rms_norm(heads(b_q), g[2]), kb, vb, tables[1], SWA_WINDOW - 1,
                                  sink=attn_sinks[layer])
        split2 = lambda t: t.reshape(b, s, HEADS_PER_MIXER, 2, DIFF_QK_DIM).transpose(0, 2, 1, 3, 4)
        lambda_init = 0.8 - 0.6 * math.exp(-0.3 * layer)
        lq1, lk1, lq2, lk2 = diff_lambda[layer, 0], diff_lambda[layer, 1], diff_lambda[layer, 2], diff_lambda[layer, 3]
        lam = jnp.exp(jnp.sum(lq1 * lk1)) - jnp.exp(jnp.sum(lq2 * lk2)) + lambda_init
        o_c = diff_attention(rms_norm(split2(c_q), qk_gain_diff[layer, 0]), rms_norm(split2(c_k), qk_gain_diff[layer, 1]),
                             heads(c_v), lam, tables[2])
        o_c = rms_norm(o_c, diff_subln[layer]) * (1.0 - lambda_init)
        gates = d_gate.reshape(b, s, HEADS_PER_MIXER, NSA_BRANCHES).transpose(0, 2, 1, 3)
        o_d = nsa_attention(rms_norm(heads(d_q), g[4]), d_kc, d_vc, d_ks, d_vs, d_kw, d_vw, gates,
                            cmp_pos[layer], cmp_w1[layer], cmp_b1[layer], cmp_w2[layer], cmp_b2[layer],
                            g[5:8], tables[3])
        y = jnp.concatenate([merge(o_a), merge(o_b), merge(o_c), merge(o_d)], axis=-1) * jax.nn.silu(silu_gate)
        x = x + y @ w_out[layer]
    return x
```

```python
import os
import math
import numpy as np
import ml_dtypes
from contextlib import ExitStack
import concourse.bass as bass
import concourse.mybir as mybir
from concourse.bass_utils import run_bass_kernel_spmd

F32 = mybir.dt.float32
BF16 = mybir.dt.bfloat16
AF = mybir.ActivationFunctionType
ALU = mybir.AluOpType
AX = mybir.AxisListType
NPBF = ml_dtypes.bfloat16

SEQ = 8192
DM = 1024
NCH = 16
NCOL = 1600
DMAXB = 3584
EPS = 1e-6
DEPTH = 2
DEBUG = os.environ.get("KDEBUG", "")
STAGE = int(os.environ.get("KSTAGE", "99"))
ATTACH_WAIT = not os.environ.get("KNOATTACH")


class Buf:
    __slots__ = ("name", "w", "r", "q")

    def __init__(self, name=""):
        self.name = name
        self.w = {}
        self.r = {}
        self.q = None


class Sched:
    def __init__(self, nc, stack):
        self.nc = nc
        self.stack = stack
        self.eng = {"pe": nc.tensor, "act": nc.scalar, "dve": nc.vector, "pool": nc.gpsimd, "sp": nc.sync}
        self.sem = {}
        self.cnt = {}
        self.waited = {}
        for k in self.eng:
            self.sem[k] = stack.enter_context(nc.semaphore("s_" + k))
            self.cnt[k] = 0
            self.waited[k] = {}
        self.nq = 0
        self.ninst = 0

    def new_queue(self, kind="q"):
        name = "%s%d" % (kind, self.nq)
        self.nq += 1
        self.sem[name] = self.stack.enter_context(self.nc.semaphore("s_" + name))
        self.cnt[name] = 0
        return name

    def _deps(self, reads, writes):
        deps = {}
        for b in reads:
            for e, i in b.w.items():
                if deps.get(e, 0) < i:
                    deps[e] = i
        for b in writes:
            for e, i in b.w.items():
                if deps.get(e, 0) < i:
                    deps[e] = i
            for e, i in b.r.items():
                if deps.get(e, 0) < i:
                    deps[e] = i
        return deps

    def _wait(self, e, deps, skip_self=False, defer_last=False):
        eng = self.eng[e]
        todo = []
        for f, i in deps.items():
            if f == e and skip_self:
                continue
            isq = f[0] == "q"
            if isq or f[0] == "c":
                i = self.cnt[f]
            if self.waited[e].get(f, 0) < i:
                todo.append((f, i * (16 if isq else 1)))
                self.waited[e][f] = i
        last = None
        if defer_last and todo and ATTACH_WAIT:
            last = todo.pop()
        for f, v in todo:
            eng.wait_ge(self.sem[f], v)
            self.ninst += 1
        return None if last is None else (self.sem[last[0]], last[1])

    def _mark(self, key, idx, reads, writes):
        for b in writes:
            b.w[key] = idx
        for b in reads:
            b.r[key] = idx

    def op(self, e, fn, reads=(), writes=()):
        last = self._wait(e, self._deps(reads, writes), skip_self=(e == "pe"), defer_last=True)
        ins = fn(self.eng[e])
        if last is not None:
            ins._wait_ge(last[0], last[1])
        self.cnt[e] += 1
        ins.then_inc(self.sem[e], 1)
        self.ninst += 1
        self._mark(e, self.cnt[e], reads, writes)
        return ins

    def dma(self, issuer, out, in_, reads=(), writes=(), qbuf=None, **kw):
        if qbuf.q is None:
            qbuf.q = self.new_queue()
        q = qbuf.q
        self._wait(issuer, self._deps(reads, writes))
        ins = self.eng[issuer].dma_start(out=out, in_=in_, **kw)
        self.cnt[q] += 1
        ins.then_inc(self.sem[q], 16)
        self.ninst += 1
        self._mark(q, self.cnt[q], reads, writes)
        return ins

    def barrier(self):
        deps = {k: v for k, v in self.cnt.items() if v > 0}
        for e in self.eng:
            self._wait(e, {k: v for k, v in deps.items() if k != e})

    def wait_bufs(self, e, bufs):
        deps = {}
        for b in bufs:
            for f, i in b.w.items():
                if deps.get(f, 0) < i:
                    deps[f] = i
        self._wait(e, deps)


def _t5_bucket(d):
    n = np.maximum(d, 0)
    nf = np.maximum(n, 1).astype(np.float32)
    large = 16 + (np.log(nf / np.float32(16)) / np.float32(math.log(128.0)) * np.float32(16)).astype(np.int32)
    large = np.minimum(large, 31)
    return np.where(n < 16, n, large)


STRIP = {
    "A": (2944, 2944 + 127, 511, 1),
    "B": (1024, 1024 + 127, 511, 1),
    "C": (2432, 2432 + 127, 511, 1),
    "W": (1408, 1408 + 127, 511, 1),
    "X": (3584, 3584 + 2032, 2063, 16),
}


def _static_consts():
    c = {}
    d = np.arange(DMAXB)
    bk = _t5_bucket(d)
    oh = np.zeros((32, DMAXB), np.float32)
    oh[bk, d] = 1.0
    c["onehot"] = oh
    for t, (sl, lw, shift, _) in STRIP.items():
        u = np.arange(lw)
        dd = u - shift
        if t == "A":
            m = ((dd >= 0) & (dd <= 128)).astype(np.float32) + ((dd >= 0) & (dd % 4 == 0) & (dd <= 512)) \
                + ((dd >= 0) & (dd % 16 == 0) & (dd <= 2048))
        elif t == "B":
            m = (dd >= 0) & (dd <= 127)
        elif t == "W":
            m = (dd >= 0) & (dd <= 511)
        else:
            m = dd >= 0
        c["cm" + t] = np.ascontiguousarray(m.astype(np.float32)[None])
    k = np.arange(SEQ)
    R = np.zeros((128, SEQ), np.float32)
    R[k // 64, k] = 1.0
    c["Rm"] = R.astype(NPBF)
    cc = np.arange(512)[:, None]
    jj = np.arange(128)[None, :]
    ov = ((16 * cc < 64 * (jj + 1)) & (16 * cc + 31 >= 64 * jj)).astype(np.float32)
    ov[511] = 0.0
    c["ovl"] = np.ascontiguousarray(ov.reshape(4, 128, 128).transpose(1, 0, 2)).astype(NPBF)
    q = np.arange(SEQ)
    cur = (q // 64)[:, None]
    forced = (jj == 0) | (jj == cur) | (jj == cur - 1)
    am = np.where(forced, np.float32(1e9), np.where(jj <= cur, np.float32(0), np.float32(-1e30))).astype(np.float32)
    c["addm"] = np.ascontiguousarray(am.reshape(64, 128, 128).transpose(1, 0, 2))
    c["ident"] = np.eye(128, dtype=np.float32).astype(NPBF)
    c["Jm"] = np.ascontiguousarray(np.eye(128, dtype=np.float32)[::-1]).astype(NPBF)
    p = np.arange(128)
    c["bones64"] = (p[:, None] // 64 == p[None, :] // 64).astype(np.float32).astype(NPBF)
    c["bones32"] = (p[:, None] // 32 == p[None, :] // 32).astype(np.float32).astype(NPBF)
    c["ones128"] = np.ones((128, 128), np.float32).astype(NPBF)
    return c


CONST_SHAPES = None


def _col_select(s):
    r = lambda a, n: list(range(a, a + n))
    cols = []
    cols += r(0 + 64 * s, 64) + r(256 + 64 * s, 64)
    cols += r(768 + 64 * s, 64) + r(1024 + 64 * (s // 2), 64)
    cols += r(1280 + 64 * s, 64) + r(1536 + 64 * s, 64)
    for i in range(4):
        cols += r(2048 + 64 * ((s + i) % 4), 64)
    cols += r(2432, 64) + r(2560, 64)
    cols += r(2304, 64) + r(2368, 64)
    for m in range(4):
        cols += r(2700 + 256 * m + 64 * s, 64)
    cols += r(2688 + 3 * s, 3) + [-1] * 125
    cols += r(512 + 64 * s, 64) + r(1152 + 64 * (s // 2), 64) + r(1792 + 64 * s, 64) + r(2496, 64) + r(2624, 64)
    assert len(cols) == NCOL
    return np.array(cols)


def _host_inputs(inputs):
    consts = _static_consts()
    f = lambda a: np.ascontiguousarray(np.asarray(a, dtype=np.float32))
    x = f(inputs["x"])
    w_in = f(inputs["w_in"])
    w_out = f(inputs["w_out"])
    tab = f(inputs["rel_bias_table"])
    norm_w = f(inputs["norm_w"])
    g = f(inputs["qk_gain"])
    gd = f(inputs["qk_gain_diff"])
    sinks = f(inputs["attn_sinks"])
    dl = f(inputs["diff_lambda"])
    subln = f(inputs["diff_subln"])
    pos = f(inputs["cmp_pos"])
    w1 = f(inputs["cmp_w1"])
    b1 = f(inputs["cmp_b1"])
    w2 = f(inputs["cmp_w2"])
    b2 = f(inputs["cmp_b2"])
    rows = np.array([m * 256 + r * 64 + i for r in range(4) for m in range(4) for i in range(64)])
    wosel = np.ascontiguousarray(w_out[:, rows, :])
    nwT = np.ascontiguousarray(norm_w.reshape(DEPTH, 8, 128).transpose(0, 2, 1))
    maps = []
    for core in range(8):
        b, s = divmod(core, 4)
        cols = _col_select(s)
        wsel = np.zeros((DEPTH, DM, NCOL), np.float32)
        ok = cols >= 0
        wsel[:, :, ok] = w_in[:, :, cols[ok]]
        gvec = np.ones((DEPTH, 128, 6), np.float32)
        for L in range(DEPTH):
            gvec[L, :, 0] = np.concatenate([g[L, 0], g[L, 1]])
            gvec[L, :, 1] = np.concatenate([g[L, 2], g[L, 3]])
            gvec[L, :, 2] = np.concatenate([gd[L, 0], gd[L, 0], gd[L, 1], gd[L, 1]])
            gvec[L, :, 3] = np.concatenate([g[L, 4], g[L, 4]])
            gvec[L, :, 4] = np.concatenate([g[L, 4], g[L, 4]])
            gvec[L, :, 5] = np.concatenate([g[L, 6], g[L, 7]])
        tcols = [s, 4 + s, 8 + s] + [12 + (s + i) % 4 for i in range(4)] + [0]
        m = {
            "xb": x[b],
            "wsel": wsel,
            "wosel": wosel,
            "nwT": nwT,
            "gvec": gvec,
            "tabsel": np.ascontiguousarray(tab[:, tcols]),
            "sink": np.ascontiguousarray(sinks[:, s].reshape(DEPTH, 1)),
            "dlam": np.ascontiguousarray(dl.reshape(DEPTH, 128)),
            "sublnT": np.ascontiguousarray(subln.reshape(DEPTH, 64, 1)),
            "kcgT": np.ascontiguousarray(g[:, 5].reshape(DEPTH, 64, 1)),
            "posT": np.ascontiguousarray(pos.transpose(0, 1, 3, 2)),
            "w1": w1,
            "b1T": np.ascontiguousarray(b1.reshape(DEPTH, 2, 2, 128).transpose(0, 1, 3, 2)),
            "w2": w2,
            "b2T": np.ascontiguousarray(b2.reshape(DEPTH, 2, 64, 1)),
        }
        m.update(consts)
        maps.append(m)
    return maps


def dram_ap(t, offset, ap):
    return bass.AP(tensor=t.tensor, offset=offset, ap=ap)


class Prog:
    def __init__(self):
        self.nc = bass.Bass("TRN2", target_bir_lowering=False)
        self.I = {}
        self.D = {}
        self.Dbuf = {}

    def din(self, name, shape, dt=F32):
        self.I[name] = self.nc.dram_tensor(name, list(shape), dt, kind="ExternalInput").ap()
        return self.I[name]

    def dscr(self, name, shape, dt, out=False):
        kind = "ExternalOutput" if (out or (DEBUG and name in DEBUG.split(","))) else "Internal"
        self.D[name] = self.nc.dram_tensor(name, list(shape), dt, kind=kind).ap()
        self.Dbuf[name] = Buf(name)
        return self.D[name]


def build_program(mode="fused"):
    P = Prog()
    P.mode = mode
    nc = P.nc
    consts = _static_consts()
    for k, v in consts.items():
        P.din(k, v.shape, BF16 if v.dtype == NPBF else F32)
    P.din("xb", [SEQ, DM])
    P.din("wsel", [DEPTH, DM, NCOL])
    P.din("wosel", [DEPTH, DM, DM])
    P.din("nwT", [DEPTH, 128, 8])
    P.din("gvec", [DEPTH, 128, 6])
    P.din("tabsel", [32, 8])
    P.din("sink", [DEPTH, 1])
    P.din("dlam", [DEPTH, 128])
    P.din("sublnT", [DEPTH, 64, 1])
    P.din("kcgT", [DEPTH, 64, 1])
    P.din("posT", [DEPTH, 2, 64, 32])
    P.din("w1", [DEPTH, 2, 2048, 256])
    P.din("b1T", [DEPTH, 2, 128, 2])
    P.din("w2", [DEPTH, 2, 256, 64])
    P.din("b2T", [DEPTH, 2, 64, 1])
    q_tok = SEQ // 4
    if mode == 2:
        out = P.dscr("out", [q_tok, DM], F32, out=True)
    else:
        out = P.dscr("out", [SEQ, DM], F32, out=(mode == "fused"))
    QT = P.dscr("QT", [8, 64, SEQ], BF16)
    KT = P.dscr("KT", [7, 64, SEQ], BF16)
    VV = P.dscr("VV", [128, 64, 512], BF16)
    GT = P.dscr("GT", [4, 64, SEQ], BF16)
    DG = P.dscr("DG", [3, SEQ], BF16)
    YTo = P.dscr("YTo", [NCH, 256, 512], BF16, out=(mode in (0, 1)))
    if mode == 1:
        YTa = P.din("YTa", [NCH, 1024, 512], BF16)
        P.Dbuf["YTa"] = Buf("YTa")
        X1 = P.dscr("X1", [SEQ, DM], F32, out=True)
    elif mode == 2:
        YTa = P.din("YTa", [NCH // 4, 1024, 512], BF16)
        P.Dbuf["YTa"] = Buf("YTa")
        X1 = P.din("X1", [q_tok, DM], F32)
        P.Dbuf["X1"] = Buf("X1")
    else:
        YTa = P.dscr("YTa", [NCH, 1024, 512], BF16)
        X1 = P.dscr("X1", [SEQ, DM], F32)
    WV = {t: P.dscr("WV" + t, [8, STRIP[t][1]], BF16) for t in STRIP}
    I, D, DB = P.I, P.D, P.Dbuf

    with ExitStack() as st0:
        S = Sched(nc, st0)
        P.S = S

        uid = [0]

        def T(name, shape, dt, stk=st0):
            uid[0] += 1
            return stk.enter_context(nc.sbuf_tensor("sb%d_%s" % (uid[0], name), list(shape), dt))

        def PS(name, shape, dt, stk):
            uid[0] += 1
            return stk.enter_context(nc.psum_tensor("pp%d_%s" % (uid[0], name), list(shape), dt))

        cbuf = Buf("consts")

        def load_const(name, shape, dt, src=None, issuer="sp"):
            t = T(name, shape, dt)
            S.dma(issuer, t[:], I[name][:] if src is None else src, writes=[cbuf], qbuf=cbuf)
            return t

        ident = load_const("ident", [128, 128], BF16)
        identf = T("identf", [128, 128], F32)
        S.op("dve", lambda e: e.tensor_copy(identf[:], ident[:]), reads=[cbuf], writes=[cbuf])
        Jm = load_const("Jm", [128, 128], BF16)
        bones64 = load_const("bones64", [128, 128], BF16)
        bones32 = load_const("bones32", [128, 128], BF16)
        ones128 = load_const("ones128", [128, 128], BF16)
        b31 = load_const("b31", [128, 8], F32, src=dram_ap(I["tabsel"], 31 * 8, [[0, 128], [1, 8]]))
        epsb = T("epsb", [128, 4], F32)
        S.op("dve", lambda e: e.memset(epsb[:, 0:1], float(DM * EPS)), writes=[cbuf])
        S.op("dve", lambda e: e.memset(epsb[:, 1:2], float(64 * EPS)), writes=[cbuf])
        S.op("dve", lambda e: e.memset(epsb[:, 2:3], float(32 * EPS)), writes=[cbuf])
        S.op("dve", lambda e: e.memset(epsb[:, 3:4], float(EPS)), writes=[cbuf])
        zt = T("zt", [128, 512], BF16)
        S.op("dve", lambda e: e.memset(zt[:], 0.0), writes=[cbuf])

        strip_specs = [("A", 0), ("B", 1), ("C", 2), ("C", 3), ("W", 3), ("X", 3), ("X", 4), ("X", 5), ("X", 6)]
        strips = [T("strip%d" % i, [128, STRIP[t][0]], BF16) for i, (t, r) in enumerate(strip_specs)]
        b_strips = Buf("strips")
        with ExitStack() as ss:
            tab_sb = T("tab_sb", [32, 8], F32, ss)
            oh_sb = T("oh_sb", [32, DMAXB], F32, ss)
            Ef = T("Ef", [8, DMAXB], F32, ss)
            bs = Buf("setup")
            S.dma("sp", tab_sb[:], I["tabsel"][:], writes=[bs], qbuf=bs)
            S.dma("sp", oh_sb[:], I["onehot"][:], writes=[bs], qbuf=bs)
            pss = [PS("pss%d" % i, [128, 512], F32, ss) for i in range(2)]
            bpss = [Buf() for _ in range(2)]
            bEf = Buf()
            for ch in range(DMAXB // 512):
                k = ch % 2
                S.op("pe", lambda e: e.matmul(pss[k][0:8, :], lhsT=tab_sb[:, :], rhs=oh_sb[:, ch * 512:(ch + 1) * 512],
                                              start=True, stop=True), reads=[bs], writes=[bpss[k]])
                S.op("act", lambda e: e.activation(out=Ef[:, ch * 512:(ch + 1) * 512], in_=pss[k][0:8, :], func=AF.Exp),
                     reads=[bpss[k]], writes=[bEf])
            for t, (sl, lw, shift, pstep) in STRIP.items():
                cm_sb = T("cm_" + t, [8, lw], F32, ss)
                wv = T("wv_" + t, [8, lw], BF16, ss)
                bt = Buf()
                S.dma("sp", cm_sb[:], dram_ap(I["cm" + t], 0, [[0, 8], [1, lw]]), writes=[bt], qbuf=bt)
                S.op("dve", lambda e: e.memset(wv[:], 0.0), writes=[bt])
                S.op("dve", lambda e: e.tensor_tensor(wv[:, shift:lw], Ef[:, 0:lw - shift], cm_sb[:, shift:lw], ALU.mult),
                     reads=[bEf, bt], writes=[bt])
                S.dma("pool", WV[t][:], wv[:], reads=[bt], writes=[DB["WV" + t]], qbuf=bt)
            hk = [T("hk%d" % i, [128, 3584], BF16, ss) for i in range(2)]
            bhk = [Buf() for _ in range(2)]
            n = 0
            for i, (t, r) in enumerate(strip_specs):
                sl, lw, shift, pstep = STRIP[t]
                h = hk[i % 2]
                S.dma("sp", h[:, 0:sl], dram_ap(WV[t], r * lw, [[pstep, 128], [1, sl]]),
                      reads=[DB["WV" + t]], writes=[bhk[i % 2]], qbuf=bhk[i % 2])
                for c0 in range(0, sl, 512):
                    w = min(512, sl - c0)
                    k = n % 2
                    n += 1
                    S.op("pe", lambda e: e.matmul(pss[k][:, 0:w], lhsT=Jm[:, :], rhs=h[:, c0:c0 + w], start=True, stop=True),
                         reads=[bhk[i % 2], cbuf], writes=[bpss[k]])
                    S.op("dve" if n % 2 else "act",
                         (lambda e: e.tensor_copy(strips[i][:, c0:c0 + w], pss[k][:, 0:w])) if n % 2 else
                         (lambda e: e.activation(out=strips[i][:, c0:c0 + w], in_=pss[k][:, 0:w], func=AF.Copy)),
                         reads=[bpss[k]], writes=[b_strips])
            S.barrier()
        P.strips = strips
        P.b_strips = b_strips

        bz = Buf()
        for c in range(NCH):
            S.dma("pool", QT[2, 32:64, c * 512:(c + 1) * 512], zt[0:32, :], reads=[cbuf], writes=[DB["QT"]], qbuf=bz)
            S.dma("pool", QT[3, 0:32, c * 512:(c + 1) * 512], zt[0:32, :], reads=[cbuf], writes=[DB["QT"]], qbuf=bz)

        def proj_phase(L, xsrc, xbuf):
            with ExitStack() as sp:
                w_bf = T("w_bf", [128, 8, NCOL], BF16, sp)
                wst = [T("wst%d" % i, [128, NCOL], F32, sp) for i in range(2)]
                b_wst = [Buf() for _ in range(2)]
                b_wbf = Buf()
                for kc in range(8):
                    k = kc % 2
                    S.dma("sp", wst[k][:], I["wsel"][L, kc * 128:(kc + 1) * 128, :], writes=[b_wst[k]], qbuf=b_wst[k])
                    S.op("pool" if k else "dve", lambda e: e.tensor_copy(w_bf[:, kc, :], wst[k][:]), reads=[b_wst[k]], writes=[b_wbf])
                nw32 = T("nw32", [128, 8], F32, sp)
                gvs = T("gvs", [128, 6], F32, sp)
                bsm = Buf()
                S.dma("sp", nw32[:], I["nwT"][L], writes=[bsm], qbuf=bsm)
                S.dma("sp", gvs[:], I["gvec"][L], writes=[bsm], qbuf=bsm)
                S.op("dve", lambda e: e.tensor_scalar(nw32[:], nw32[:], 32.0, None, ALU.mult), reads=[bsm], writes=[bsm])
                S.op("dve", lambda e: e.tensor_scalar(gvs[:], gvs[:], 8.0, None, ALU.mult), reads=[bsm], writes=[bsm])
                S.op("dve", lambda e: e.tensor_scalar(gvs[:, 2:3], gvs[:, 2:3], math.sqrt(32.0) / 8.0, None, ALU.mult),
                     reads=[bsm], writes=[bsm])
                NX = 8
                xt = [T("xt%d" % i, [128, DM], F32, sp) for i in range(NX)]
                junk = T("junk", [128, DM], BF16, sp)
                ssq = [T("ssq%d" % i, [128, 1], F32, sp) for i in range(NX)]
                xn = [T("xn%d" % i, [128, DM], BF16, sp) for i in range(NX)]
                xnT = [T("xnT%d" % i, [128, 8, 512], BF16, sp) for i in range(2)]
                pst = [PS("pst%d" % i, [128, 8, 128], BF16, sp) for i in range(2)]
                NH = 4
                psh = [PS("psh%d" % i, [128, 512], F32, sp) for i in range(NH)]
                psn = [PS("psn0", [128, 512], F32, sp)] * NH
                psv = [PS("psv0", [128, 512], F32, sp)] * 2
                sq = [T("sq%d" % i, [128, 512], BF16, sp) for i in range(NH)]
                rs = [T("rs%d" % i, [128, 512], F32, sp) for i in range(NH)]
                NO = 4
                ot = [T("ot%d" % i, [128, 512], BF16, sp) for i in range(NO)]
                vt = [T("vt%d" % i, [128, 512], BF16, sp) for i in range(2)]
                dgt = [T("dgt%d" % i, [3, 512], F32, sp) for i in range(2)] * 2
                dgb = [T("dgb%d" % i, [3, 512], BF16, sp) for i in range(2)] * 2
                B = lambda n: [Buf() for _ in range(n)]
                b_xt, b_ss, b_xn, b_xnT, b_pst = B(NX), B(NX), B(NX), B(2), B(2)
                b_psh, b_psn, b_psv, b_sq, b_rs, b_ot, b_vt, b_dgt = B(NH), [Buf()] * NH, [Buf()] * 2, B(NH), B(NH), B(NO), B(2), B(2) * 2
                b_junk = Buf()
                cnt = {"h": 0, "o": 0, "v": 0, "x": 0}
                for i in range(2):
                    S.op("dve", lambda e: e.memset(vt[i][:], 1.0), writes=[b_vt[i]])

                def prepA(c):
                    for t in range(4):
                        tile = 4 * c + t
                        k = tile % NX
                        S.dma("sp", xt[k][:], xsrc[tile * 128:(tile + 1) * 128, :], reads=[xbuf], writes=[b_xt[k]], qbuf=b_xt[k])
                        S.op("act", lambda e: e.activation(out=junk[:], in_=xt[k][:], func=AF.Square, accum_out=ssq[k][:]),
                             reads=[b_xt[k]], writes=[b_junk, b_ss[k]])
                        S.op("act", lambda e: e.activation(out=ssq[k][:], in_=ssq[k][:], func=AF.Ln, bias=epsb[:, 0:1]),
                             reads=[b_ss[k], cbuf], writes=[b_ss[k]])
                        S.op("act", lambda e: e.activation(out=ssq[k][:], in_=ssq[k][:], func=AF.Exp, scale=-0.5),
                             reads=[b_ss[k]], writes=[b_ss[k]])
                        S.op("dve", lambda e: e.tensor_scalar(xn[k][:], xt[k][:], ssq[k][:, 0:1], None, ALU.mult),
                             reads=[b_xt[k], b_ss[k]], writes=[b_xn[k]])

                def prepB(c):
                    for t in range(4):
                        tile = 4 * c + t
                        k = tile % NX
                        k2 = tile % 2
                        for kc in range(8):
                            S.op("pe", lambda e: e.transpose(pst[k2][:, kc, :], xn[k][:, kc * 128:(kc + 1) * 128], ident[:, :]),
                                 reads=[b_xn[k], cbuf], writes=[b_pst[k2]])
                        S.op("dve", lambda e: e.tensor_tensor(
                            xnT[c % 2][:, :, t * 128:(t + 1) * 128], pst[k2][:, :, :],
                            nw32[:, :, None].to_broadcast([128, 8, 128]), ALU.mult),
                            reads=[b_pst[k2], bsm], writes=[b_xnT[c % 2]])

                def store(src_ap, dst_ap, bsrc, dname):
                    S.dma("pool", dst_ap, src_ap, reads=[bsrc], writes=[DB[dname]], qbuf=bsrc)

                def blocks(c):
                    cs = slice(c * 512, (c + 1) * 512)
                    xT = xnT[c % 2]
                    pendb = []

                    def part1(blk):
                        k = cnt["h"] % NH
                        cnt["h"] += 1
                        for kc in range(8):
                            S.op("pe", lambda e: e.matmul(psh[k][:, :], lhsT=w_bf[:, kc, blk * 128:(blk + 1) * 128], rhs=xT[:, kc, :],
                                                          start=(kc == 0), stop=(kc == 7)),
                                 reads=[b_wbf, b_xnT[c % 2]], writes=[b_psh[k]])
                        if blk <= 5:
                            S.op("act", lambda e: e.activation(out=sq[k][:], in_=psh[k][:], func=AF.Square),
                                 reads=[b_psh[k]], writes=[b_sq[k]])
                        return k

                    def part2(blk, k):
                        if blk == 9:
                            S.op("act", lambda e: e.activation(out=dgt[k][:], in_=psh[k][0:3, :], func=AF.Exp, scale=-1.0),
                                 reads=[b_psh[k]], writes=[b_dgt[k]])
                            S.op("dve", lambda e: e.tensor_scalar(dgt[k][:], dgt[k][:], 1.0, None, ALU.add),
                                 reads=[b_dgt[k]], writes=[b_dgt[k]])
                            S.op("dve", lambda e: e.reciprocal(dgt[k][:], dgt[k][:]), reads=[b_dgt[k]], writes=[b_dgt[k]])
                            S.op("dve", lambda e: e.tensor_copy(dgb[k][:], dgt[k][:]), reads=[b_dgt[k]], writes=[b_dgt[k]])
                            store(dgb[k][:], DG[:, cs], b_dgt[k], "DG")
                            return
                        o = cnt["o"] % NO
                        cnt["o"] += 1
                        if blk <= 5:
                            dh = 32 if blk == 2 else 64
                            bon = bones32 if blk == 2 else bones64
                            S.op("pe", lambda e: e.matmul(psn[k][:, :], lhsT=bon[:, :], rhs=sq[k][:], start=True, stop=True),
                                 reads=[b_sq[k], cbuf], writes=[b_psn[k]])
                            S.op("act", lambda e: e.activation(out=rs[k][:], in_=psn[k][:], func=AF.Ln,
                                                               bias=epsb[:, 2:3] if dh == 32 else epsb[:, 1:2]),
                                 reads=[b_psn[k], cbuf], writes=[b_rs[k]])
                            S.op("act", lambda e: e.activation(out=rs[k][:], in_=rs[k][:], func=AF.Exp, scale=-0.5),
                                 reads=[b_rs[k]], writes=[b_rs[k]])
                            S.op("dve", lambda e: e.scalar_tensor_tensor(ot[o][:], psh[k][:], gvs[:, blk:blk + 1], rs[k][:],
                                                                         ALU.mult, ALU.mult),
                                 reads=[b_psh[k], b_rs[k], bsm], writes=[b_ot[o]])
                            if blk == 0:
                                store(ot[o][0:64, :], QT[0, :, cs], b_ot[o], "QT")
                                store(ot[o][64:128, :], KT[0, :, cs], b_ot[o], "KT")
                            elif blk == 1:
                                store(ot[o][0:64, :], QT[1, :, cs], b_ot[o], "QT")
                                store(ot[o][64:128, :], KT[1, :, cs], b_ot[o], "KT")
                            elif blk == 2:
                                store(ot[o][0:32, :], QT[2, 0:32, cs], b_ot[o], "QT")
                                store(ot[o][32:64, :], QT[3, 32:64, cs], b_ot[o], "QT")
                                store(ot[o][64:128, :], KT[2, :, cs], b_ot[o], "KT")
                            elif blk in (3, 4):
                                store(ot[o][0:64, :], QT[4 + 2 * (blk - 3), :, cs], b_ot[o], "QT")
                                store(ot[o][64:128, :], QT[5 + 2 * (blk - 3), :, cs], b_ot[o], "QT")
                            else:
                                store(ot[o][0:64, :], KT[3, :, cs], b_ot[o], "KT")
                                store(ot[o][64:128, :], KT[4, :, cs], b_ot[o], "KT")
                        elif blk == 6:
                            S.op("act", lambda e: e.activation(out=ot[o][:], in_=psh[k][:], func=AF.Copy),
                                 reads=[b_psh[k]], writes=[b_ot[o]])
                            store(ot[o][0:64, :], KT[5, :, cs], b_ot[o], "KT")
                            store(ot[o][64:128, :], KT[6, :, cs], b_ot[o], "KT")
                        else:
                            S.op("act", lambda e: e.activation(out=rs[k][:], in_=psh[k][:], func=AF.Exp, scale=-1.0),
                                 reads=[b_psh[k]], writes=[b_rs[k]])
                            S.op("dve", lambda e: e.tensor_scalar(rs[k][:], rs[k][:], 1.0, None, ALU.add),
                                 reads=[b_rs[k]], writes=[b_rs[k]])
                            S.op("dve", lambda e: e.reciprocal(rs[k][:], rs[k][:]), reads=[b_rs[k]], writes=[b_rs[k]])
                            S.op("dve", lambda e: e.tensor_tensor(ot[o][:], psh[k][:], rs[k][:], ALU.mult),
                                 reads=[b_psh[k], b_rs[k]], writes=[b_ot[o]])
                            store(ot[o][0:64, :], GT[2 * (blk - 7), :, cs], b_ot[o], "GT")
                            store(ot[o][64:128, :], GT[2 * (blk - 7) + 1, :, cs], b_ot[o], "GT")

                    pq = []
                    for blk in range(10):
                        k = part1(blk)
                        pq.append((blk, k))
                        if len(pq) > 1:
                            part2(*pq.pop(0))
                    while pq:
                        part2(*pq.pop(0))

                def vblocks(c):
                    xT = xnT[c % 2]
                    for t in range(4):
                        k = cnt["v"] % 2
                        cnt["v"] += 1
                        for kc in range(8):
                            S.op("pe", lambda e: e.matmul(psv[k][:, 0:320], lhsT=xT[:, kc, t * 128:(t + 1) * 128], rhs=w_bf[:, kc, 1280:1600],
                                                          start=(kc == 0), stop=(kc == 7)),
                                 reads=[b_wbf, b_xnT[c % 2]], writes=[b_psv[k]])
                        S.op("dve", lambda e: e.tensor_copy(vt[k][:, 0:64], psv[k][:, 0:64]), reads=[b_psv[k]], writes=[b_vt[k]])
                        S.op("dve", lambda e: e.tensor_copy(vt[k][:, 128:256], psv[k][:, 64:192]), reads=[b_psv[k]], writes=[b_vt[k]])
                        S.op("dve", lambda e: e.tensor_copy(vt[k][:, 320:448], psv[k][:, 192:320]), reads=[b_psv[k]], writes=[b_vt[k]])
                        store(vt[k][:], VV[:, 4 * c + t, :], b_vt[k], "VV")

                nchunks = NCH if STAGE > 1 else 2
                prepA(0)
                prepB(0)
                for c in range(nchunks):
                    if c + 1 < nchunks:
                        prepA(c + 1)
                    blocks(c)
                    if c + 1 < nchunks:
                        prepB(c + 1)
                    vblocks(c)
                S.barrier()


        kcn = T("kcn", [64, 512], BF16)
        vcx = T("vcx", [128, 4, 64], BF16)
        b_kcn, b_vcx = Buf("kcn"), Buf("vcx")

        def compress_phase(L):
            with ExitStack() as sc:
                xT = T("c_xT", [64, SEQ], BF16, sc)
                xs = T("c_xs", [64, 32, 512], BF16, sc)
                bxs = Buf()
                w1s = [T("c_w1s%d" % i, [64, 8, 256], F32, sc) for i in range(2)]
                w1b = T("c_w1b", [64, 32, 256], BF16, sc)
                w2s = T("c_w2s", [128, 2, 64], F32, sc)
                w2b = T("c_w2b", [128, 2, 64], BF16, sc)
                pTs = T("c_pTs", [64, 32], F32, sc)
                pTb = T("c_pTb", [64, 32], BF16, sc)
                b1 = T("c_b1", [128, 2], F32, sc)
                b2 = T("c_b2", [64, 1], F32, sc)
                kg = T("c_kg", [64, 1], F32, sc)
                u = T("c_u", [128, 512], F32, sc)
                u2 = T("c_u2", [128, 512], F32, sc)
                gl = T("c_gl", [128, 2, 512], BF16, sc)
                kc = T("c_kc", [64, 512], F32, sc)
                sqc = T("c_sq", [64, 512], BF16, sc)
                rsc = T("c_rs", [64, 512], F32, sc)
                vcT = T("c_vcT", [64, 512], BF16, sc)
                ps_h = [PS("c_psh%d" % i, [128, 512], F32, sc) for i in range(2)]
                ps_pb = PS("c_pspb", [128, 512], F32, sc)
                ps_c = PS("c_psc", [128, 512], F32, sc)
                ps_t = PS("c_pst", [128, 4, 64], BF16, sc)
                bx, bw1s, bw, bu, bph, bpb, bpc, bpt, bkc = Buf(), [Buf(), Buf()], Buf(), Buf(), [Buf(), Buf()], Buf(), Buf(), Buf(), Buf()
                S.dma("sp", kg[:], I["kcgT"][L], writes=[bw], qbuf=bw)
                S.op("dve", lambda e: e.tensor_scalar(kg[:], kg[:], 8.0, None, ALU.mult), reads=[bw], writes=[bw])
                S.op("dve", lambda e: e.memset(kcn[:], 0.0), writes=[b_kcn])
                S.op("dve", lambda e: e.memset(vcT[:], 0.0), writes=[bkc])
                for kv in range(2):
                    S.dma("sp", xT[:], KT[5 + kv], reads=[DB["KT"]], writes=[bx], qbuf=bx)
                    for q4 in range(4):
                        k = q4 % 2
                        S.dma("sp", w1s[k][:], I["w1"][L, kv, q4 * 512:(q4 + 1) * 512, :].rearrange("(t d) h -> d t h", d=64),
                              writes=[bw1s[k]], qbuf=bw1s[k])
                        S.op("dve", lambda e: e.tensor_copy(w1b[:, q4 * 8:(q4 + 1) * 8, :], w1s[k][:]), reads=[bw1s[k]], writes=[bw])
                    S.dma("sp", w2s[:], I["w2"][L, kv].rearrange("(c p) d -> p c d", p=128), writes=[bw], qbuf=bw)
                    S.op("dve", lambda e: e.tensor_copy(w2b[:], w2s[:]), reads=[bw], writes=[bw])
                    S.dma("sp", pTs[:], I["posT"][L, kv], writes=[bw], qbuf=bw)
                    S.op("dve", lambda e: e.tensor_copy(pTb[:], pTs[:]), reads=[bw], writes=[bw])
                    S.dma("sp", b1[:], I["b1T"][L, kv], writes=[bw], qbuf=bw)
                    S.dma("sp", b2[:], I["b2T"][L, kv], writes=[bw], qbuf=bw)
                    for hc in range(2):
                        for t in range(32):
                            S.op("pe", lambda e: e.matmul(ps_pb[:, hc:hc + 1], lhsT=w1b[:, t, hc * 128:(hc + 1) * 128], rhs=pTb[:, t:t + 1],
                                                          start=(t == 0), stop=(t == 31)), reads=[bw], writes=[bpb])
                    S.op("dve", lambda e: e.tensor_tensor(b1[:], b1[:], ps_pb[:, 0:2], ALU.add), reads=[bpb, bw], writes=[bw])
                    for t in range(32):
                        S.op("pool" if t % 2 else "dve", lambda e: e.tensor_copy(xs[:, t, 0:511], xT[:, t:t + 16 * 510 + 1:16]),
                             reads=[bx], writes=[bxs])
                    for hc in range(2):
                        for t in range(32):
                            S.op("pe", lambda e: e.matmul(ps_h[hc][:, 0:511], lhsT=w1b[:, t, hc * 128:(hc + 1) * 128],
                                                          rhs=xs[:, t, 0:511], start=(t == 0), stop=(t == 31)),
                                 reads=[bw, bxs], writes=[bph[hc]])
                        S.op("dve", lambda e: e.tensor_scalar(u[:, 0:511], ps_h[hc][:, 0:511], b1[:, hc:hc + 1], None, ALU.add),
                             reads=[bph[hc], bw], writes=[bu])
                        S.op("dve", lambda e: e.tensor_tensor(u2[:, 0:511], u[:, 0:511], u[:, 0:511], ALU.mult), reads=[bu], writes=[bu])
                        S.op("dve", lambda e: e.tensor_scalar(u2[:, 0:511], u2[:, 0:511], 0.044715, 1.0, ALU.mult, ALU.add), reads=[bu], writes=[bu])
                        S.op("dve", lambda e: e.tensor_tensor(u2[:, 0:511], u2[:, 0:511], u[:, 0:511], ALU.mult), reads=[bu], writes=[bu])
                        S.op("act", lambda e: e.activation(out=u2[:, 0:511], in_=u2[:, 0:511], func=AF.Exp, scale=-1.5957691216057308),
                             reads=[bu], writes=[bu])
                        S.op("dve", lambda e: e.tensor_scalar(u2[:, 0:511], u2[:, 0:511], 1.0, None, ALU.add), reads=[bu], writes=[bu])
                        S.op("dve", lambda e: e.reciprocal(u2[:, 0:511], u2[:, 0:511]), reads=[bu], writes=[bu])
                        S.op("dve", lambda e: e.tensor_tensor(gl[:, hc, 0:511], u[:, 0:511], u2[:, 0:511], ALU.mult), reads=[bu], writes=[bu])
                    for hc in range(2):
                        S.op("pe", lambda e: e.matmul(ps_c[0:64, 0:511], lhsT=w2b[:, hc, :], rhs=gl[:, hc, 0:511],
                                                      start=(hc == 0), stop=(hc == 1)), reads=[bw, bu], writes=[bpc])
                    if kv == 0:
                        S.op("dve", lambda e: e.tensor_scalar(kc[:, 0:511], ps_c[0:64, 0:511], b2[:, 0:1], None, ALU.add),
                             reads=[bpc, bw], writes=[bkc])
                        S.op("act", lambda e: e.activation(out=sqc[:, 0:511], in_=kc[:, 0:511], func=AF.Square), reads=[bkc], writes=[bkc])
                        S.op("pe", lambda e: e.matmul(ps_c[0:64, 0:511], lhsT=bones64[0:64, 0:64], rhs=sqc[:, 0:511], start=True, stop=True),
                             reads=[bkc, cbuf], writes=[bpc])
                        S.op("act", lambda e: e.activation(out=rsc[:, 0:511], in_=ps_c[0:64, 0:511], func=AF.Ln, bias=epsb[0:64, 1:2]),
                             reads=[bpc, cbuf], writes=[bkc])
                        S.op("act", lambda e: e.activation(out=rsc[:, 0:511], in_=rsc[:, 0:511], func=AF.Exp, scale=-0.5), reads=[bkc], writes=[bkc])
                        S.op("dve", lambda e: e.scalar_tensor_tensor(kcn[:, 0:511], kc[:, 0:511], kg[:, 0:1], rsc[:, 0:511], ALU.mult, ALU.mult),
                             reads=[bkc, bw], writes=[b_kcn])
                    else:
                        S.op("dve", lambda e: e.tensor_scalar(vcT[:, 0:511], ps_c[0:64, 0:511], b2[:, 0:1], None, ALU.add),
                             reads=[bpc, bw], writes=[bkc])
                        for cb in range(4):
                            S.op("pe", lambda e: e.transpose(ps_t[:, cb, :], vcT[:, cb * 128:(cb + 1) * 128], ident[0:64, 0:64]),
                                 reads=[bkc, cbuf], writes=[bpt])
                        S.op("dve", lambda e: e.tensor_copy(vcx[:], ps_t[:]), reads=[bpt], writes=[b_vcx])
                S.barrier()

        cq = S.new_queue("c")

        def attn_phase(L):
            lam_init = 0.8 - 0.6 * math.exp(-0.3 * L)
            with ExitStack() as sa:
                Vall = T("Vall", [128, 64, 512], BF16, sa)
                KTs = T("KTs", [128, SEQ], BF16, sa)
                Rm = T("Rm_sb", [128, SEQ], BF16, sa)
                ovl = T("ovl_sb", [128, 4, 128], BF16, sa)
                bV, bK, bR = Buf("Vall"), [Buf("K0"), Buf("K1")], Buf("R")
                S.dma("sp", Vall[:], VV[:], reads=[DB["VV"]], writes=[bV], qbuf=bV)
                S.dma("sp", KTs[0:64, :], KT[0], reads=[DB["KT"]], writes=[bK[0]], qbuf=bK[0])
                S.dma("sp", KTs[64:128, :], KT[1], reads=[DB["KT"]], writes=[bK[1]], qbuf=bK[1])
                S.dma("sp", Rm[:], I["Rm"][:], writes=[bR], qbuf=bR)
                S.dma("sp", ovl[:], I["ovl"][:], writes=[bR], qbuf=bR)
                sm = Buf("attn_small")
                esink = T("esink", [128, 1], F32, sa)
                S.dma("sp", esink[:], dram_ap(I["sink"], L, [[0, 128], [1, 1]]), writes=[sm], qbuf=sm)
                S.op("act", lambda e: e.activation(out=esink[:], in_=esink[:], func=AF.Exp), reads=[sm], writes=[sm])
                dl = T("dl", [128, 4, 32], F32, sa)
                S.dma("sp", dl[:], dram_ap(I["dlam"], L * 128, [[0, 128], [32, 4], [1, 32]]), writes=[sm], qbuf=sm)
                pr = T("dlp", [128, 2, 32], F32, sa)
                s12 = T("s12", [128, 2], F32, sa)
                neglam = T("neglam", [128, 1], F32, sa)
                S.op("dve", lambda e: e.tensor_tensor(pr[:, 0, :], dl[:, 0, :], dl[:, 1, :], ALU.mult), reads=[sm], writes=[sm])
                S.op("dve", lambda e: e.tensor_tensor(pr[:, 1, :], dl[:, 2, :], dl[:, 3, :], ALU.mult), reads=[sm], writes=[sm])
                S.op("dve", lambda e: e.reduce_sum(out=s12[:], in_=pr[:], axis=AX.X), reads=[sm], writes=[sm])
                S.op("act", lambda e: e.activation(out=s12[:], in_=s12[:], func=AF.Exp), reads=[sm], writes=[sm])
                S.op("dve", lambda e: e.tensor_tensor(neglam[:], s12[:, 1:2], s12[:, 0:1], ALU.subtract), reads=[sm], writes=[sm])
                S.op("dve", lambda e: e.tensor_scalar(neglam[:], neglam[:], -lam_init, None, ALU.add), reads=[sm], writes=[sm])
                sgC = T("sgC", [64, 1], F32, sa)
                S.dma("sp", sgC[:], I["sublnT"][L], writes=[sm], qbuf=sm)
                S.op("dve", lambda e: e.tensor_scalar(sgC[:], sgC[:], 8.0 * (1.0 - lam_init), None, ALU.mult), reads=[sm], writes=[sm])

                NP_ = 5
                pt = [T("pt%d" % i, [128, 512], BF16, sa) for i in range(NP_)]
                b_pt = [Buf() for _ in range(NP_)]
                ps_s = [PS("ps_s%d" % i, [128, 512], F32, sa) for i in range(4)]
                ps_o = [PS("ps_o%d" % i, [128, 512], F32, sa) for i in range(2)]
                ps_x = PS("ps_x", [128, 512], F32, sa)
                ps_imp = PS("ps_imp", [128, 4, 128], F32, sa)
                b_ps_s, b_ps_o = [Buf(), Buf(), Buf(), Buf()], [Buf(), Buf()]
                b_ps_x, b_ps_imp = Buf(), Buf()
                ps_tr, b_ps_tr = ps_imp, b_ps_imp
                ps_oc, b_ps_oc = ps_x, b_ps_x
                NQ = 3
                qt = [T("qt%d" % i, [128, 2, 512], BF16, sa) for i in range(NQ)]
                gtl = [T("gtl%d" % i, [64, 512], BF16, sa) for i in range(NQ)]
                b_qt = [Buf() for _ in range(NQ)]
                rd = [T("rd%d" % i, [128, 512], F32, sa) for i in range(2)]
                fo = [T("fo%d" % i, [64, 512], F32, sa) for i in range(3)]
                yo = [T("yo%d" % i, [64, 512], BF16, sa) for i in range(2)]
                sqa = T("sqa", [64, 512], BF16, sa)
                b_fin = Buf("fin")
                b_yo = [Buf(), Buf()]
                cn = {"s": 0, "p": 0, "q": 0, "y": 0, "dve": 0}

                pend = []
                DEPTH_LA = 3

                def flush():
                    while pend:
                        pend.pop(0)()

                def tile_step(kt_ap, q_ap, bias_ap, strip_ap, vx_ap, pso, bpso, first, last, kbuf, qbuf_, extra=None, after=None, mask_eng=None):
                    k = cn["s"] % 4
                    cn["s"] += 1
                    p = cn["p"] % NP_
                    cn["p"] += 1
                    scale = tile_step.scale
                    S.op("pe", lambda e: e.matmul(ps_s[k][:, :], lhsT=kt_ap, rhs=q_ap, start=True, stop=(extra is None)),
                         reads=[kbuf, qbuf_], writes=[b_ps_s[k]])
                    if extra is not None:
                        S.op("pe", lambda e: e.matmul(ps_s[k][:, :], lhsT=extra[0], rhs=extra[1], start=False, stop=True),
                             reads=[bR, extra[2]], writes=[b_ps_s[k]])
                    if bias_ap is None:
                        S.op("act", lambda e: e.activation(out=pt[p][:], in_=ps_s[k][:], func=AF.Exp, scale=scale),
                             reads=[b_ps_s[k]], writes=[b_pt[p]])
                    else:
                        S.op("act", lambda e: e.activation(out=pt[p][:], in_=ps_s[k][:], func=AF.Exp, scale=scale, bias=bias_ap),
                             reads=[b_ps_s[k], cbuf], writes=[b_pt[p]])
                    if strip_ap is not None:
                        cn["dve"] += 1
                        eng = mask_eng if mask_eng is not None else ("pool" if cn["dve"] % 4 == 0 else "dve")
                        S.op(eng, lambda e: e.tensor_tensor(pt[p][:], pt[p][:], strip_ap, ALU.mult),
                             reads=[b_pt[p], b_strips], writes=[b_pt[p]])

                    def pv_part():
                        S.op("pe", lambda e: e.matmul(pso[:, :], lhsT=vx_ap, rhs=pt[p][:], start=first, stop=last),
                             reads=[bV, b_pt[p]], writes=[bpso])
                        if after is not None:
                            after()
                    pend.append(pv_part)
                    if len(pend) > DEPTH_LA:
                        pend.pop(0)()

                def store_y(c, m, acc_ap, gate_ap, q_b):
                    y = cn["y"] % 2
                    cn["y"] += 1
                    S.op("dve", lambda e: e.tensor_tensor(yo[y][:], acc_ap, gate_ap, ALU.mult), reads=[b_fin, q_b], writes=[b_yo[y]])
                    S.dma("pool", YTo[c, m * 64:(m + 1) * 64, :], yo[y][:], reads=[b_yo[y]], writes=[DB["YTo"]], qbuf=b_yo[y])

                def load_q(c, qidx_list, pbs, m):
                    i = cn["q"] % NQ
                    cn["q"] += 1
                    cs = slice(c * 512, (c + 1) * 512)
                    for j, (qi, pb) in enumerate(zip(qidx_list, pbs)):
                        S.dma("sp", qt[i][pb:pb + 64, j, :], QT[qi, :, cs], reads=[DB["QT"]], writes=[b_qt[i]], qbuf=b_qt[i])
                    S.dma("sp", gtl[i][:], GT[m, :, cs], reads=[DB["GT"]], writes=[b_qt[i]], qbuf=b_qt[i])
                    return i

                nch = NCH if STAGE > 3 else 2

                tile_step.scale = 0.125
                for c in range(nch):
                    i = load_q(c, [0], [0], 0)
                    kbs = list(range(max(0, 4 * c - 16), 4 * c + 4))

                    def finA(c=c, i=i):
                        po = ps_o[c % 2]
                        S.op("dve", lambda e: e.reciprocal(rd[0][64:128, :], po[64:128, :]), reads=[b_ps_o[c % 2]], writes=[b_fin])
                        S.op("dve", lambda e: e.tensor_tensor(fo[0][:], po[0:64, :], rd[0][64:128, :], ALU.mult), reads=[b_ps_o[c % 2], b_fin], writes=[b_fin])
                        store_y(c, 0, fo[0][:], gtl[i][:], b_qt[i])
                    for n, kb in enumerate(kbs):
                        dlt = 512 * c - 128 * kb
                        tile_step(KTs[0:64, kb * 128:(kb + 1) * 128], qt[i][0:64, 0, :], None,
                                  strips[0][:, dlt + 384:dlt + 896], Vall[:, kb, 0:128], ps_o[c % 2], b_ps_o[c % 2],
                                  n == 0, n == len(kbs) - 1, bK[0], b_qt[i], after=finA if n == len(kbs) - 1 else None)
                flush()

                if STAGE > 2:
                    S.dma("sp", KTs[0:64, :], KT[2], reads=[DB["KT"]], writes=[bK[0]], qbuf=bK[0])
                for c in range(nch):
                    i = load_q(c, [1], [64], 1)
                    kbs = list(range(max(0, 4 * c - 1), 4 * c + 4))

                    def finB(c=c, i=i):
                        po = ps_o[c % 2]
                        S.op("dve", lambda e: e.tensor_scalar(rd[0][0:64, :], po[0:64, :], esink[0:64, 0:1], None, ALU.add),
                             reads=[b_ps_o[c % 2], sm], writes=[b_fin])
                        S.op("dve", lambda e: e.reciprocal(rd[0][0:64, :], rd[0][0:64, :]), reads=[b_fin], writes=[b_fin])
                        S.op("dve", lambda e: e.tensor_tensor(fo[0][:], po[64:128, :], rd[0][0:64, :], ALU.mult), reads=[b_ps_o[c % 2], b_fin], writes=[b_fin])
                        store_y(c, 1, fo[0][:], gtl[i][:], b_qt[i])
                    for n, kb in enumerate(kbs):
                        dlt = 512 * c - 128 * kb
                        tile_step(KTs[64:128, kb * 128:(kb + 1) * 128], qt[i][64:128, 0, :], None,
                                  strips[1][:, dlt + 384:dlt + 896], Vall[:, kb, 64:192], ps_o[c % 2], b_ps_o[c % 2],
                                  n == 0, n == len(kbs) - 1, bK[1], b_qt[i], after=finB if n == len(kbs) - 1 else None)
                flush()

                if STAGE > 2:
                    S.dma("sp", KTs[64:128, :], KT[4], reads=[DB["KT"]], writes=[bK[1]], qbuf=bK[1])
                    tile_step.scale = 32.0 ** -0.5
                    for c in range(nch):
                        i = load_q(c, [2, 3], [0, 0], 2)
                        kbs = list(range(0, 4 * c + 4))

                        def finC(c=c, i=i):
                            S.op("dve", lambda e: e.reciprocal(rd[0][64:128, :], ps_o[0][64:128, :]), reads=[b_ps_o[0]], writes=[b_fin])
                            S.op("dve", lambda e: e.reciprocal(rd[1][64:128, :], ps_o[1][64:128, :]), reads=[b_ps_o[1]], writes=[b_fin])
                            S.op("dve", lambda e: e.tensor_tensor(fo[0][:], ps_o[0][0:64, :], rd[0][64:128, :], ALU.mult), reads=[b_ps_o[0], b_fin], writes=[b_fin])
                            S.op("dve", lambda e: e.tensor_tensor(fo[1][:], ps_o[1][0:64, :], rd[1][64:128, :], ALU.mult), reads=[b_ps_o[1], b_fin], writes=[b_fin])
                            S.op("dve", lambda e: e.scalar_tensor_tensor(fo[0][:], fo[1][:], neglam[0:64, 0:1], fo[0][:], ALU.mult, ALU.add),
                                 reads=[b_fin, sm], writes=[b_fin])
                            S.op("act", lambda e: e.activation(out=sqa[:], in_=fo[0][:], func=AF.Square), reads=[b_fin], writes=[b_fin])
                            S.op("pe", lambda e: e.matmul(ps_x[0:64, :], lhsT=bones64[0:64, 0:64], rhs=sqa[:], start=True, stop=True),
                                 reads=[b_fin, cbuf], writes=[b_ps_x])
                            S.op("act", lambda e: e.activation(out=fo[1][:], in_=ps_x[0:64, :], func=AF.Ln, bias=epsb[0:64, 1:2]),
                                 reads=[b_ps_x, cbuf], writes=[b_fin])
                            S.op("act", lambda e: e.activation(out=fo[1][:], in_=fo[1][:], func=AF.Exp, scale=-0.5), reads=[b_fin], writes=[b_fin])
                            S.op("dve", lambda e: e.scalar_tensor_tensor(fo[0][:], fo[0][:], sgC[:, 0:1], fo[1][:], ALU.mult, ALU.mult),
                                 reads=[b_fin, sm], writes=[b_fin])
                            store_y(c, 2, fo[0][:], gtl[i][:], b_qt[i])
                        for n, kb in enumerate(kbs):
                            dlt = 512 * c - 128 * kb
                            near = dlt <= 1536
                            for mp in range(2):
                                tile_step(KTs[32 * mp:32 * mp + 32, kb * 128:(kb + 1) * 128], qt[i][32 * mp:32 * mp + 32, mp, :],
                                          None if near else b31[:, 2:3],
                                          strips[2][:, dlt + 384:dlt + 896] if near else None,
                                          Vall[:, kb, 192:320], ps_o[mp], b_ps_o[mp], n == 0, n == len(kbs) - 1, bK[0], b_qt[i],
                                          after=finC if (n == len(kbs) - 1 and mp == 1) else None)
                    flush()

                if STAGE > 2:
                    S.dma("sp", KTs[0:64, :], KT[3], reads=[DB["KT"]], writes=[bK[0]], qbuf=bK[0])
                    tile_step.scale = 0.125
                    qd = [T("qd%d" % i, [64, 4, 512], BF16, sa) for i in range(2)]
                    gb = [T("gb%d" % i, [64, 3, 512], BF16, sa) for i in range(2)]
                    am = [T("am0", [128, 4, 128], F32, sa)] * 2
                    b_am = Buf()
                    b_qd = [Buf(), Buf()]
                    eb = [T("eb%d" % i, [128, 4, 512], BF16, sa) for i in range(2)]
                    b_eb = [Buf(), Buf()]
                    im = T("im", [128, 4, 128], F32, sa)
                    wk = T("wk", [128, 128], F32, sa)
                    m8 = T("m8", [128, 16], F32, sa)
                    nsb = im
                    nselT = [T("nselT%d" % i, [128, 512], BF16, sa) for i in range(2)]
                    fo2 = [fo[2], T("fo3", [64, 512], F32, sa)]
                    b_tk, b_nsel, b_fo2 = Buf(), [Buf(), Buf()], [Buf(), Buf()]
                    info = {}

                    def loadsD(c):
                        cs = slice(c * 512, (c + 1) * 512)
                        d = c % 2
                        for hd in range(4):
                            S.dma("sp", qd[d][:, hd, :], QT[4 + hd, :, cs], reads=[DB["QT"]], writes=[b_qd[d]], qbuf=b_qd[d])
                        for br in range(3):
                            S.dma("sp", gb[d][:, br, :], dram_ap(DG, br * SEQ + c * 512, [[0, 64], [1, 512]]),
                                  reads=[DB["DG"]], writes=[b_qd[d]], qbuf=b_qd[d])
                        S.dma("sp", am[d][:], I["addm"][:, 4 * c:4 * c + 4, :], writes=[b_am], qbuf=b_am)
                        i = load_q(c, [4], [64], 3)
                        info[c] = (d, i)

                    def cmp_topk(c):
                        d, i = info[c]
                        ncb = c // 4 + 1
                        for hd in range(4):
                            ee = eb[hd % 2]
                            bee = b_eb[hd % 2]
                            for cb in range(ncb):
                                m = c - 4 * cb
                                far = m >= 7
                                k = cn["s"] % 4
                                cn["s"] += 1
                                S.op("pe", lambda e: e.matmul(ps_s[k][:, :], lhsT=kcn[:, cb * 128:(cb + 1) * 128], rhs=qd[d][:, hd, :],
                                                              start=True, stop=True), reads=[b_kcn, b_qd[d]], writes=[b_ps_s[k]])
                                if far:
                                    S.op("act", lambda e: e.activation(out=ee[:, cb, :], in_=ps_s[k][:], func=AF.Exp, scale=0.125,
                                                                       bias=b31[:, 3 + hd:4 + hd]), reads=[b_ps_s[k], cbuf], writes=[bee])
                                else:
                                    S.op("act", lambda e: e.activation(out=ee[:, cb, :], in_=ps_s[k][:], func=AF.Exp, scale=0.125),
                                         reads=[b_ps_s[k]], writes=[bee])
                                    S.op("dve", lambda e: e.tensor_tensor(ee[:, cb, :], ee[:, cb, :], strips[5 + hd][:, 512 * m:512 * m + 512], ALU.mult),
                                         reads=[bee, b_strips], writes=[bee])
                            yield
                            for cb in range(ncb):
                                S.op("pe", lambda e: e.matmul(ps_x[:, :], lhsT=ones128[:, :], rhs=ee[:, cb, :], start=(cb == 0), stop=(cb == ncb - 1)),
                                     reads=[bee, cbuf], writes=[b_ps_x])
                            S.op("dve", lambda e: e.tensor_scalar(rd[1][:], ps_x[:], 1e-30, None, ALU.max), reads=[b_ps_x], writes=[b_fin])
                            S.op("dve", lambda e: e.reciprocal(rd[1][:], rd[1][:]), reads=[b_fin], writes=[b_fin])
                            yield
                            for cb in range(ncb):
                                S.op("dve", lambda e: e.tensor_tensor(ee[:, cb, :], ee[:, cb, :], rd[1][:], ALU.mult), reads=[bee, b_fin], writes=[bee])
                            yield
                            if hd == 0:
                                for cb in range(ncb):
                                    S.op("pe", lambda e: e.matmul(ps_oc[0:64, :], lhsT=vcx[:, cb, :], rhs=ee[:, cb, :], start=(cb == 0), stop=(cb == ncb - 1)),
                                         reads=[bee, b_vcx], writes=[b_ps_oc])
                                S.op("dve", lambda e: e.tensor_tensor(fo2[d][:], ps_oc[0:64, :], gb[d][:, 0, :], ALU.mult),
                                     reads=[b_ps_oc, b_qd[d]], writes=[b_fo2[d]])
                            for t in range(4):
                                for cb in range(ncb):
                                    S.op("pe", lambda e: e.matmul(ps_imp[:, t, :], lhsT=ee[:, cb, t * 128:(t + 1) * 128], rhs=ovl[:, cb, :],
                                                                  start=(hd == 0 and cb == 0), stop=(hd == 3 and cb == ncb - 1)),
                                         reads=[bee, bR], writes=[b_ps_imp])
                            yield
                        for t in range(4):
                            S.op("dve", lambda e: e.tensor_tensor(im[:, t, :], ps_imp[:, t, :], am[d][:, t, :], ALU.add), reads=[b_ps_imp, b_am], writes=[b_tk])
                        yield
                        for t in range(4):
                            S.op("dve", lambda e: e.max(out=m8[:, 0:8], in_=im[:, t, :]), reads=[b_tk], writes=[b_tk])
                            S.op("dve", lambda e: e.match_replace(out=wk[:], in_to_replace=m8[:, 0:8], in_values=im[:, t, :], imm_value=-3.0e38),
                                 reads=[b_tk], writes=[b_tk])
                            S.op("dve", lambda e: e.max(out=m8[:, 8:16], in_=wk[:]), reads=[b_tk], writes=[b_tk])
                            S.op("dve", lambda e: e.tensor_scalar(nsb[:, t, :], im[:, t, :], m8[:, 15:16], -1024.0, ALU.is_lt, ALU.mult), reads=[b_tk], writes=[b_tk])
                            S.op("pe", lambda e: e.transpose(ps_tr[:, t, :], nsb[:, t, :], identf[:, :]), reads=[b_tk, cbuf], writes=[b_ps_tr])
                            yield
                        for t in range(4):
                            S.op("dve", lambda e: e.tensor_copy(nselT[d][:, t * 128:(t + 1) * 128], ps_tr[:, t, :]), reads=[b_ps_tr], writes=[b_nsel[d]])
                        yield

                    def step(bg):
                        if bg[0] is not None:
                            try:
                                next(bg[0])
                            except StopIteration:
                                bg[0] = None

                    loadsD(0)
                    for _ in cmp_topk(0):
                        pass
                    for c in range(nch):
                        d, i = info[c]
                        bg = [None]
                        if c + 1 < nch:
                            loadsD(c + 1)
                            bg[0] = cmp_topk(c + 1)

                        def finW(d=d):
                            S.op("dve", lambda e: e.reciprocal(rd[0][64:128, :], ps_o[1][64:128, :]), reads=[b_ps_o[1]], writes=[b_fin])
                            S.op("dve", lambda e: e.tensor_tensor(fo[0][:], ps_o[1][0:64, :], rd[0][64:128, :], ALU.mult), reads=[b_ps_o[1], b_fin], writes=[b_fin])
                            S.op("dve", lambda e: e.tensor_tensor(fo[0][:], fo[0][:], gb[d][:, 2, :], ALU.mult), reads=[b_fin, b_qd[d]], writes=[b_fin])
                            S.op("dve", lambda e: e.tensor_tensor(fo2[d][:], fo2[d][:], fo[0][:], ALU.add), reads=[b_fin, b_fo2[d]], writes=[b_fo2[d]])
                        kbs = list(range(max(0, 4 * c - 4), 4 * c + 4))
                        for n, kb in enumerate(kbs):
                            dlt = 512 * c - 128 * kb
                            tile_step(KTs[64:128, kb * 128:(kb + 1) * 128], qt[i][64:128, 0, :], None,
                                      strips[4][:, dlt + 384:dlt + 896], Vall[:, kb, 384:512], ps_o[1], b_ps_o[1],
                                      n == 0, n == len(kbs) - 1, bK[1], b_qt[i], after=finW if n == len(kbs) - 1 else None)
                            step(bg)

                        def finS(c=c, d=d, i=i):
                            S.op("dve", lambda e: e.reciprocal(rd[0][0:64, :], ps_o[0][0:64, :]), reads=[b_ps_o[0]], writes=[b_fin])
                            S.op("dve", lambda e: e.tensor_tensor(fo[0][:], ps_o[0][64:128, :], rd[0][0:64, :], ALU.mult), reads=[b_ps_o[0], b_fin], writes=[b_fin])
                            S.op("dve", lambda e: e.tensor_tensor(fo[0][:], fo[0][:], gb[d][:, 1, :], ALU.mult), reads=[b_fin, b_qd[d]], writes=[b_fin])
                            S.op("dve", lambda e: e.tensor_tensor(fo2[d][:], fo2[d][:], fo[0][:], ALU.add), reads=[b_fin, b_fo2[d]], writes=[b_fo2[d]])
                            y = cn["y"] % 2
                            cn["y"] += 1
                            S.op("dve", lambda e: e.tensor_tensor(yo[y][:], fo2[d][:], gtl[i][:], ALU.mult), reads=[b_fo2[d], b_qt[i]], writes=[b_yo[y]])
                            S.dma("pool", YTo[c, 192:256, :], yo[y][:], reads=[b_yo[y]], writes=[DB["YTo"]], qbuf=b_yo[y])
                        kbs = list(range(0, 4 * c + 4))
                        for n, kb in enumerate(kbs):
                            dlt = 512 * c - 128 * kb
                            near = dlt <= 1536
                            tile_step(KTs[0:64, kb * 128:(kb + 1) * 128], qd[d][:, 0, :], None if near else b31[:, 3:4],
                                      strips[3][:, dlt + 384:dlt + 896] if near else None, Vall[:, kb, 256:384], ps_o[0], b_ps_o[0],
                                      n == 0, n == len(kbs) - 1, bK[0], b_qd[d], extra=(Rm[:, kb * 128:(kb + 1) * 128], nselT[d][:], b_nsel[d]),
                                      after=finS if n == len(kbs) - 1 else None)
                            step(bg)
                        while bg[0] is not None:
                            step(bg)
                        flush()
                        if mode == "fused":
                            S._wait("pool", S._deps([DB["YTo"]], [DB["YTa"]]))
                            ins = nc.gpsimd.collective_compute("AllGather", op=ALU.bypass, replica_groups=[[0, 1, 2, 3], [4, 5, 6, 7]],
                                                               ins=[YTo[c]], outs=[YTa[c]])
                            S.cnt[cq] += 1
                            ins.then_inc(S.sem[cq], 1)
                            S.ninst += 1
                            S._mark(cq, S.cnt[cq], [DB["YTo"]], [DB["YTa"]])
                S.barrier()

        def outproj_phase(L, xsrc, xbuf, dst, dname, gather=True, nchk=NCH):
            with ExitStack() as so:
                wo = T("wo", [128, 8, DM], BF16, so)
                wos = [T("wos%d" % i, [128, DM], F32, so) for i in range(2)]
                bwos, bwo = [Buf(), Buf()], Buf()
                for kc in range(8):
                    k = kc % 2
                    S.dma("sp", wos[k][:], I["wosel"][L, kc * 128:(kc + 1) * 128, :], writes=[bwos[k]], qbuf=bwos[k])
                    S.op("pool" if k else "dve", lambda e: e.tensor_copy(wo[:, kc, :], wos[k][:]), reads=[bwos[k]], writes=[bwo])
                yT = [T("yT%d" % i, [128, 8, 512], BF16, so) for i in range(2)]
                xr = [T("xr%d" % i, [128, DM], F32, so) for i in range(3)]
                psq = [PS("psq%d" % i, [128, 512], F32, so) for i in range(4)]
                byT, bxr, bpsq = [Buf(), Buf()], [Buf(), Buf(), Buf()], [Buf() for _ in range(4)]
                n = 0
                for c in range(nchk):
                    y = yT[c % 2]
                    S.dma("sp", y[:], YTa[c].rearrange("(f p) t -> p f t", p=128), reads=[DB["YTa"]], writes=[byT[c % 2]], qbuf=byT[c % 2])
                    for t in range(4):
                        tile = 4 * c + t
                        x3 = tile % 3
                        S.dma("sp", xr[x3][:], xsrc[tile * 128:(tile + 1) * 128, :], reads=[xbuf], writes=[bxr[x3]], qbuf=bxr[x3])
                        for hf in range(2):
                            k = n % 4
                            n += 1
                            for fc in range(8):
                                S.op("pe", lambda e: e.matmul(psq[k][:, :], lhsT=y[:, fc, t * 128:(t + 1) * 128], rhs=wo[:, fc, hf * 512:(hf + 1) * 512],
                                                              start=(fc == 0), stop=(fc == 7)), reads=[byT[c % 2], bwo], writes=[bpsq[k]])
                            S.op("dve", lambda e: e.tensor_tensor(xr[x3][:, hf * 512:(hf + 1) * 512], xr[x3][:, hf * 512:(hf + 1) * 512], psq[k][:, :], ALU.add),
                                 reads=[bpsq[k], bxr[x3]], writes=[bxr[x3]])
                        S.dma("pool", dst[tile * 128:(tile + 1) * 128, :], xr[x3][:], reads=[bxr[x3]], writes=[DB[dname]], qbuf=bxr[x3])
                S.barrier()

        P.proj_phase = proj_phase
        xin_buf = Buf("xin")
        if mode == "fused":
            proj_phase(0, I["xb"], xin_buf)
            if STAGE > 1:
                compress_phase(0)
                attn_phase(0)
            if STAGE > 3:
                outproj_phase(0, I["xb"], xin_buf, X1, "X1")
                proj_phase(1, X1, DB["X1"])
                compress_phase(1)
                attn_phase(1)
                outproj_phase(1, X1, DB["X1"], out, "out")
        elif mode == 0:
            proj_phase(0, I["xb"], xin_buf)
            compress_phase(0)
            attn_phase(0)
        elif mode == 1:
            outproj_phase(0, I["xb"], xin_buf, X1, "X1", gather=False)
            proj_phase(1, X1, DB["X1"])
            compress_phase(1)
            attn_phase(1)
        else:
            outproj_phase(1, X1, DB["X1"], out, "out", gather=False, nchk=NCH // 4)
        S.barrier()
    print("instructions:", S.ninst, {k: v for k, v in S.cnt.items() if not k.startswith("q")}, "queues", S.nq)
    return P


_PROGS = {}


def _prog(mode):
    if mode not in _PROGS:
        _PROGS[mode] = build_program(mode)
    return _PROGS[mode]


def _run(mode, maps):
    P = _prog(mode)
    ms = [{k: v for k, v in m.items() if k in P.I} for m in maps]
    return run_bass_kernel_spmd(P.nc, ms, core_ids=list(range(8)))


def kernel(**inputs):
    maps = _host_inputs(inputs)
    if not os.environ.get("KMULTI"):
        res = _run("fused", maps)
        o = np.stack([np.asarray(res.results[0]["out"]), np.asarray(res.results[4]["out"])], axis=0)
        return o.astype(np.float32)
    r0 = _run(0, maps)
    for b in range(2):
        yta = np.concatenate([np.asarray(r0.results[4 * b + s]["YTo"]) for s in range(4)], axis=1)
        for s in range(4):
            maps[4 * b + s]["YTa"] = yta
    del r0
    r1 = _run(1, maps)
    qt = SEQ // 4
    for b in range(2):
        yta = np.concatenate([np.asarray(r1.results[4 * b + s]["YTo"]) for s in range(4)], axis=1)
        x1 = np.asarray(r1.results[4 * b]["X1"])
        for s in range(4):
            maps[4 * b + s]["YTa"] = np.ascontiguousarray(yta[4 * s:4 * s + 4])
            maps[4 * b + s]["X1"] = np.ascontiguousarray(x1[s * qt:(s + 1) * qt])
    del r1
    r2 = _run(2, maps)
    o = np.stack([np.concatenate([np.asarray(r2.results[4 * b + s]["out"]) for s in range(4)], axis=0) for b in range(2)], axis=0)
    return o.astype(np.float32)
```

```python
import os
import math
import numpy as np
import ml_dtypes
from contextlib import ExitStack
import concourse.bass as bass
import concourse.mybir as mybir
from concourse.bass_utils import run_bass_kernel_spmd

F32 = mybir.dt.float32
BF16 = mybir.dt.bfloat16
AF = mybir.ActivationFunctionType
ALU = mybir.AluOpType
AX = mybir.AxisListType
NPBF = ml_dtypes.bfloat16

SEQ = 8192
DM = 1024
NCH = 16
NCOL = 1600
DMAXB = 3584
EPS = 1e-6
DEPTH = 2
DEBUG = os.environ.get("KDEBUG", "")
STAGE = int(os.environ.get("KSTAGE", "99"))
ATTACH_WAIT = not os.environ.get("KNOATTACH")


class Buf:
    __slots__ = ("name", "w", "r", "q")

    def __init__(self, name=""):
        self.name = name
        self.w = {}
        self.r = {}
        self.q = None


class Sched:
    def __init__(self, nc, stack):
        self.nc = nc
        self.stack = stack
        self.eng = {"pe": nc.tensor, "act": nc.scalar, "dve": nc.vector, "pool": nc.gpsimd, "sp": nc.sync}
        self.sem = {}
        self.cnt = {}
        self.waited = {}
        for k in self.eng:
            self.sem[k] = stack.enter_context(nc.semaphore("s_" + k))
            self.cnt[k] = 0
            self.waited[k] = {}
        self.nq = 0
        self.ninst = 0

    def new_queue(self, kind="q"):
        name = "%s%d" % (kind, self.nq)
        self.nq += 1
        self.sem[name] = self.stack.enter_context(self.nc.semaphore("s_" + name))
        self.cnt[name] = 0
        return name

    def _deps(self, reads, writes):
        deps = {}
        for b in reads:
            for e, i in b.w.items():
                if deps.get(e, 0) < i:
                    deps[e] = i
        for b in writes:
            for e, i in b.w.items():
                if deps.get(e, 0) < i:
                    deps[e] = i
            for e, i in b.r.items():
                if deps.get(e, 0) < i:
                    deps[e] = i
        return deps

    def _wait(self, e, deps, skip_self=False, defer_last=False):
        eng = self.eng[e]
        todo = []
        for f, i in deps.items():
            if f == e and skip_self:
                continue
            isq = f[0] == "q"
            if isq or f[0] == "c":
                i = self.cnt[f]
            if self.waited[e].get(f, 0) < i:
                todo.append((f, i * (16 if isq else 1)))
                self.waited[e][f] = i
        last = None
        if defer_last and todo and ATTACH_WAIT:
            last = todo.pop()
        for f, v in todo:
            eng.wait_ge(self.sem[f], v)
            self.ninst += 1
        return None if last is None else (self.sem[last[0]], last[1])

    def _mark(self, key, idx, reads, writes):
        for b in writes:
            b.w[key] = idx
        for b in reads:
            b.r[key] = idx

    def op(self, e, fn, reads=(), writes=()):
        last = self._wait(e, self._deps(reads, writes), skip_self=(e == "pe"), defer_last=True)
        ins = fn(self.eng[e])
        if last is not None:
            ins._wait_ge(last[0], last[1])
        self.cnt[e] += 1
        ins.then_inc(self.sem[e], 1)
        self.ninst += 1
        self._mark(e, self.cnt[e], reads, writes)
        return ins

    def dma(self, issuer, out, in_, reads=(), writes=(), qbuf=None, **kw):
        if qbuf.q is None:
            qbuf.q = self.new_queue()
        q = qbuf.q
        self._wait(issuer, self._deps(reads, writes))
        ins = self.eng[issuer].dma_start(out=out, in_=in_, **kw)
        self.cnt[q] += 1
        ins.then_inc(self.sem[q], 16)
        self.ninst += 1
        self._mark(q, self.cnt[q], reads, writes)
        return ins

    def barrier(self):
        deps = {k: v for k, v in self.cnt.items() if v > 0}
        for e in self.eng:
            self._wait(e, {k: v for k, v in deps.items() if k != e})

    def wait_bufs(self, e, bufs):
        deps = {}
        for b in bufs:
            for f, i in b.w.items():
                if deps.get(f, 0) < i:
                    deps[f] = i
        self._wait(e, deps)


def _t5_bucket(d):
    n = np.maximum(d, 0)
    nf = np.maximum(n, 1).astype(np.float32)
    large = 16 + (np.log(nf / np.float32(16)) / np.float32(math.log(128.0)) * np.float32(16)).astype(np.int32)
    large = np.minimum(large, 31)
    return np.where(n < 16, n, large)


STRIP = {
    "A": (2944, 2944 + 127, 511, 1),
    "B": (1024, 1024 + 127, 511, 1),
    "C": (2432, 2432 + 127, 511, 1),
    "W": (1408, 1408 + 127, 511, 1),
    "X": (3584, 3584 + 2032, 2063, 16),
}


def _static_consts():
    c = {}
    d = np.arange(DMAXB)
    bk = _t5_bucket(d)
    oh = np.zeros((32, DMAXB), np.float32)
    oh[bk, d] = 1.0
    c["onehot"] = oh
    for t, (sl, lw, shift, _) in STRIP.items():
        u = np.arange(lw)
        dd = u - shift
        if t == "A":
            m = ((dd >= 0) & (dd <= 128)).astype(np.float32) + ((dd >= 0) & (dd % 4 == 0) & (dd <= 512)) \
                + ((dd >= 0) & (dd % 16 == 0) & (dd <= 2048))
        elif t == "B":
            m = (dd >= 0) & (dd <= 127)
        elif t == "W":
            m = (dd >= 0) & (dd <= 511)
        else:
            m = dd >= 0
        c["cm" + t] = np.ascontiguousarray(m.astype(np.float32)[None])
    k = np.arange(SEQ)
    R = np.zeros((128, SEQ), np.float32)
    R[k // 64, k] = 1.0
    c["Rm"] = R.astype(NPBF)
    cc = np.arange(512)[:, None]
    jj = np.arange(128)[None, :]
    ov = ((16 * cc < 64 * (jj + 1)) & (16 * cc + 31 >= 64 * jj)).astype(np.float32)
    ov[511] = 0.0
    c["ovl"] = np.ascontiguousarray(ov.reshape(4, 128, 128).transpose(1, 0, 2)).astype(NPBF)
    q = np.arange(SEQ)
    cur = (q // 64)[:, None]
    forced = (jj == 0) | (jj == cur) | (jj == cur - 1)
    am = np.where(forced, np.float32(1e9), np.where(jj <= cur, np.float32(0), np.float32(-1e30))).astype(np.float32)
    c["addm"] = np.ascontiguousarray(am.reshape(64, 128, 128).transpose(1, 0, 2))
    c["ident"] = np.eye(128, dtype=np.float32).astype(NPBF)
    c["Jm"] = np.ascontiguousarray(np.eye(128, dtype=np.float32)[::-1]).astype(NPBF)
    p = np.arange(128)
    c["bones64"] = (p[:, None] // 64 == p[None, :] // 64).astype(np.float32).astype(NPBF)
    c["bones32"] = (p[:, None] // 32 == p[None, :] // 32).astype(np.float32).astype(NPBF)
    c["ones128"] = np.ones((128, 128), np.float32).astype(NPBF)
    return c


CONST_SHAPES = None


def _col_select(s):
    r = lambda a, n: list(range(a, a + n))
    cols = []
    cols += r(0 + 64 * s, 64) + r(256 + 64 * s, 64)
    cols += r(768 + 64 * s, 64) + r(1024 + 64 * (s // 2), 64)
    cols += r(1280 + 64 * s, 64) + r(1536 + 64 * s, 64)
    for i in range(4):
        cols += r(2048 + 64 * ((s + i) % 4), 64)
    cols += r(2432, 64) + r(2560, 64)
    cols += r(2304, 64) + r(2368, 64)
    for m in range(4):
        cols += r(2700 + 256 * m + 64 * s, 64)
    cols += r(2688 + 3 * s, 3) + [-1] * 125
    cols += r(512 + 64 * s, 64) + r(1152 + 64 * (s // 2), 64) + r(1792 + 64 * s, 64) + r(2496, 64) + r(2624, 64)
    assert len(cols) == NCOL
    return np.array(cols)


def _host_inputs(inputs):
    consts = _static_consts()
    f = lambda a: np.ascontiguousarray(np.asarray(a, dtype=np.float32))
    x = f(inputs["x"])
    w_in = f(inputs["w_in"])
    w_out = f(inputs["w_out"])
    tab = f(inputs["rel_bias_table"])
    norm_w = f(inputs["norm_w"])
    g = f(inputs["qk_gain"])
    gd = f(inputs["qk_gain_diff"])
    sinks = f(inputs["attn_sinks"])
    dl = f(inputs["diff_lambda"])
    subln = f(inputs["diff_subln"])
    pos = f(inputs["cmp_pos"])
    w1 = f(inputs["cmp_w1"])
    b1 = f(inputs["cmp_b1"])
    w2 = f(inputs["cmp_w2"])
    b2 = f(inputs["cmp_b2"])
    rows = np.array([m * 256 + r * 64 + i for r in range(4) for m in range(4) for i in range(64)])
    wosel = np.ascontiguousarray(w_out[:, rows, :])
    nwT = np.ascontiguousarray(norm_w.reshape(DEPTH, 8, 128).transpose(0, 2, 1))
    maps = []
    for core in range(8):
        b, s = divmod(core, 4)
        cols = _col_select(s)
        wsel = np.zeros((DEPTH, DM, NCOL), np.float32)
        ok = cols >= 0
        wsel[:, :, ok] = w_in[:, :, cols[ok]]
        gvec = np.ones((DEPTH, 128, 6), np.float32)
        for L in range(DEPTH):
            gvec[L, :, 0] = np.concatenate([g[L, 0], g[L, 1]])
            gvec[L, :, 1] = np.concatenate([g[L, 2], g[L, 3]])
            gvec[L, :, 2] = np.concatenate([gd[L, 0], gd[L, 0], gd[L, 1], gd[L, 1]])
            gvec[L, :, 3] = np.concatenate([g[L, 4], g[L, 4]])
            gvec[L, :, 4] = np.concatenate([g[L, 4], g[L, 4]])
            gvec[L, :, 5] = np.concatenate([g[L, 6], g[L, 7]])
        tcols = [s, 4 + s, 8 + s] + [12 + (s + i) % 4 for i in range(4)] + [0]
        m = {
            "xb": x[b],
            "wsel": wsel,
            "wosel": wosel,
            "nwT": nwT,
            "gvec": gvec,
            "tabsel": np.ascontiguousarray(tab[:, tcols]),
            "sink": np.ascontiguousarray(sinks[:, s].reshape(DEPTH, 1)),
            "dlam": np.ascontiguousarray(dl.reshape(DEPTH, 128)),
            "sublnT": np.ascontiguousarray(subln.reshape(DEPTH, 64, 1)),
            "kcgT": np.ascontiguousarray(g[:, 5].reshape(DEPTH, 64, 1)),
            "posT": np.ascontiguousarray(pos.transpose(0, 1, 3, 2)),
            "w1": w1,
            "b1T": np.ascontiguousarray(b1.reshape(DEPTH, 2, 2, 128).transpose(0, 1, 3, 2)),
            "w2": w2,
            "b2T": np.ascontiguousarray(b2.reshape(DEPTH, 2, 64, 1)),
        }
        m.update(consts)
        maps.append(m)
    return maps


def dram_ap(t, offset, ap):
    return bass.AP(tensor=t.tensor, offset=offset, ap=ap)


class Prog:
    def __init__(self):
        self.nc = bass.Bass("TRN2", target_bir_lowering=False)
        self.I = {}
        self.D = {}
        self.Dbuf = {}

    def din(self, name, shape, dt=F32):
        self.I[name] = self.nc.dram_tensor(name, list(shape), dt, kind="ExternalInput").ap()
        return self.I[name]

    def dscr(self, name, shape, dt, out=False):
        kind = "ExternalOutput" if (out or (DEBUG and name in DEBUG.split(","))) else "Internal"
        self.D[name] = self.nc.dram_tensor(name, list(shape), dt, kind=kind).ap()
        self.Dbuf[name] = Buf(name)
        return self.D[name]


def build_program(mode="fused"):
    P = Prog()
    P.mode = mode
    nc = P.nc
    consts = _static_consts()
    for k, v in consts.items():
        P.din(k, v.shape, BF16 if v.dtype == NPBF else F32)
    P.din("xb", [SEQ, DM])
    P.din("wsel", [DEPTH, DM, NCOL])
    P.din("wosel", [DEPTH, DM, DM])
    P.din("nwT", [DEPTH, 128, 8])
    P.din("gvec", [DEPTH, 128, 6])
    P.din("tabsel", [32, 8])
    P.din("sink", [DEPTH, 1])
    P.din("dlam", [DEPTH, 128])
    P.din("sublnT", [DEPTH, 64, 1])
    P.din("kcgT", [DEPTH, 64, 1])
    P.din("posT", [DEPTH, 2, 64, 32])
    P.din("w1", [DEPTH, 2, 2048, 256])
    P.din("b1T", [DEPTH, 2, 128, 2])
    P.din("w2", [DEPTH, 2, 256, 64])
    P.din("b2T", [DEPTH, 2, 64, 1])
    q_tok = SEQ // 4
    if mode == 2:
        out = P.dscr("out", [q_tok, DM], F32, out=True)
    else:
        out = P.dscr("out", [SEQ, DM], F32, out=(mode == "fused"))
    QT = P.dscr("QT", [8, 64, SEQ], BF16)
    KT = P.dscr("KT", [7, 64, SEQ], BF16)
    VV = P.dscr("VV", [128, 64, 512], BF16)
    GT = P.dscr("GT", [4, 64, SEQ], BF16)
    DG = P.dscr("DG", [3, SEQ], BF16)
    YTo = P.dscr("YTo", [NCH, 256, 512], BF16, out=(mode in (0, 1)))
    if mode == 1:
        YTa = P.din("YTa", [NCH, 1024, 512], BF16)
        P.Dbuf["YTa"] = Buf("YTa")
        X1 = P.dscr("X1", [SEQ, DM], F32, out=True)
    elif mode == 2:
        YTa = P.din("YTa", [NCH // 4, 1024, 512], BF16)
        P.Dbuf["YTa"] = Buf("YTa")
        X1 = P.din("X1", [q_tok, DM], F32)
        P.Dbuf["X1"] = Buf("X1")
    else:
        YTa = P.dscr("YTa", [NCH, 1024, 512], BF16)
        X1 = P.dscr("X1", [SEQ, DM], F32)
    WV = {t: P.dscr("WV" + t, [8, STRIP[t][1]], BF16) for t in STRIP}
    I, D, DB = P.I, P.D, P.Dbuf

    with ExitStack() as st0:
        S = Sched(nc, st0)
        P.S = S

        uid = [0]

        def T(name, shape, dt, stk=st0):
            uid[0] += 1
            return stk.enter_context(nc.sbuf_tensor("sb%d_%s" % (uid[0], name), list(shape), dt))

        def PS(name, shape, dt, stk):
            uid[0] += 1
            return stk.enter_context(nc.psum_tensor("pp%d_%s" % (uid[0], name), list(shape), dt))

        cbuf = Buf("consts")

        def load_const(name, shape, dt, src=None, issuer="sp"):
            t = T(name, shape, dt)
            S.dma(issuer, t[:], I[name][:] if src is None else src, writes=[cbuf], qbuf=cbuf)
            return t

        ident = load_const("ident", [128, 128], BF16)
        identf = T("identf", [128, 128], F32)
        S.op("dve", lambda e: e.tensor_copy(identf[:], ident[:]), reads=[cbuf], writes=[cbuf])
        Jm = load_const("Jm", [128, 128], BF16)
        bones64 = load_const("bones64", [128, 128], BF16)
        bones32 = load_const("bones32", [128, 128], BF16)
        ones128 = load_const("ones128", [128, 128], BF16)
        b31 = load_const("b31", [128, 8], F32, src=dram_ap(I["tabsel"], 31 * 8, [[0, 128], [1, 8]]))
        epsb = T("epsb", [128, 4], F32)
        S.op("dve", lambda e: e.memset(epsb[:, 0:1], float(DM * EPS)), writes=[cbuf])
        S.op("dve", lambda e: e.memset(epsb[:, 1:2], float(64 * EPS)), writes=[cbuf])
        S.op("dve", lambda e: e.memset(epsb[:, 2:3], float(32 * EPS)), writes=[cbuf])
        S.op("dve", lambda e: e.memset(epsb[:, 3:4], float(EPS)), writes=[cbuf])
        zt = T("zt", [128, 512], BF16)
        S.op("dve", lambda e: e.memset(zt[:], 0.0), writes=[cbuf])

        strip_specs = [("A", 0), ("B", 1), ("C", 2), ("C", 3), ("W", 3), ("X", 3), ("X", 4), ("X", 5), ("X", 6)]
        strips = [T("strip%d" % i, [128, STRIP[t][0]], BF16) for i, (t, r) in enumerate(strip_specs)]
        b_strips = Buf("strips")
        with ExitStack() as ss:
            tab_sb = T("tab_sb", [32, 8], F32, ss)
            oh_sb = T("oh_sb", [32, DMAXB], F32, ss)
            Ef = T("Ef", [8, DMAXB], F32, ss)
            bs = Buf("setup")
            S.dma("sp", tab_sb[:], I["tabsel"][:], writes=[bs], qbuf=bs)
            S.dma("sp", oh_sb[:], I["onehot"][:], writes=[bs], qbuf=bs)
            pss = [PS("pss%d" % i, [128, 512], F32, ss) for i in range(2)]
            bpss = [Buf() for _ in range(2)]
            bEf = Buf()
            for ch in range(DMAXB // 512):
                k = ch % 2
                S.op("pe", lambda e: e.matmul(pss[k][0:8, :], lhsT=tab_sb[:, :], rhs=oh_sb[:, ch * 512:(ch + 1) * 512],
                                              start=True, stop=True), reads=[bs], writes=[bpss[k]])
                S.op("act", lambda e: e.activation(out=Ef[:, ch * 512:(ch + 1) * 512], in_=pss[k][0:8, :], func=AF.Exp),
                     reads=[bpss[k]], writes=[bEf])
            for t, (sl, lw, shift, pstep) in STRIP.items():
                cm_sb = T("cm_" + t, [8, lw], F32, ss)
                wv = T("wv_" + t, [8, lw], BF16, ss)
                bt = Buf()
                S.dma("sp", cm_sb[:], dram_ap(I["cm" + t], 0, [[0, 8], [1, lw]]), writes=[bt], qbuf=bt)
                S.op("dve", lambda e: e.memset(wv[:], 0.0), writes=[bt])
                S.op("dve", lambda e: e.tensor_tensor(wv[:, shift:lw], Ef[:, 0:lw - shift], cm_sb[:, shift:lw], ALU.mult),
                     reads=[bEf, bt], writes=[bt])
                S.dma("pool", WV[t][:], wv[:], reads=[bt], writes=[DB["WV" + t]], qbuf=bt)
            hk = [T("hk%d" % i, [128, 3584], BF16, ss) for i in range(2)]
            bhk = [Buf() for _ in range(2)]
            n = 0
            for i, (t, r) in enumerate(strip_specs):
                sl, lw, shift, pstep = STRIP[t]
                h = hk[i % 2]
                S.dma("sp", h[:, 0:sl], dram_ap(WV[t], r * lw, [[pstep, 128], [1, sl]]),
                      reads=[DB["WV" + t]], writes=[bhk[i % 2]], qbuf=bhk[i % 2])
                for c0 in range(0, sl, 512):
                    w = min(512, sl - c0)
                    k = n % 2
                    n += 1
                    S.op("pe", lambda e: e.matmul(pss[k][:, 0:w], lhsT=Jm[:, :], rhs=h[:, c0:c0 + w], start=True, stop=True),
                         reads=[bhk[i % 2], cbuf], writes=[bpss[k]])
                    S.op("dve" if n % 2 else "act",
                         (lambda e: e.tensor_copy(strips[i][:, c0:c0 + w], pss[k][:, 0:w])) if n % 2 else
                         (lambda e: e.activation(out=strips[i][:, c0:c0 + w], in_=pss[k][:, 0:w], func=AF.Copy)),
                         reads=[bpss[k]], writes=[b_strips])
            S.barrier()
        P.strips = strips
        P.b_strips = b_strips

        bz = Buf()
        for c in range(NCH):
            S.dma("pool", QT[2, 32:64, c * 512:(c + 1) * 512], zt[0:32, :], reads=[cbuf], writes=[DB["QT"]], qbuf=bz)
            S.dma("pool", QT[3, 0:32, c * 512:(c + 1) * 512], zt[0:32, :], reads=[cbuf], writes=[DB["QT"]], qbuf=bz)

        def proj_phase(L, xsrc, xbuf):
            with ExitStack() as sp:
                w_bf = T("w_bf", [128, 8, NCOL], BF16, sp)
                wst = [T("wst%d" % i, [128, NCOL], F32, sp) for i in range(2)]
                b_wst = [Buf() for _ in range(2)]
                b_wbf = Buf()
                for kc in range(8):
                    k = kc % 2
                    S.dma("sp", wst[k][:], I["wsel"][L, kc * 128:(kc + 1) * 128, :], writes=[b_wst[k]], qbuf=b_wst[k])
                    S.op("pool" if k else "dve", lambda e: e.tensor_copy(w_bf[:, kc, :], wst[k][:]), reads=[b_wst[k]], writes=[b_wbf])
                nw32 = T("nw32", [128, 8], F32, sp)
                gvs = T("gvs", [128, 6], F32, sp)
                bsm = Buf()
                S.dma("sp", nw32[:], I["nwT"][L], writes=[bsm], qbuf=bsm)
                S.dma("sp", gvs[:], I["gvec"][L], writes=[bsm], qbuf=bsm)
                S.op("dve", lambda e: e.tensor_scalar(nw32[:], nw32[:], 32.0, None, ALU.mult), reads=[bsm], writes=[bsm])
                S.op("dve", lambda e: e.tensor_scalar(gvs[:], gvs[:], 8.0, None, ALU.mult), reads=[bsm], writes=[bsm])
                S.op("dve", lambda e: e.tensor_scalar(gvs[:, 2:3], gvs[:, 2:3], math.sqrt(32.0) / 8.0, None, ALU.mult),
                     reads=[bsm], writes=[bsm])
                NX = 8
                xt = [T("xt%d" % i, [128, DM], F32, sp) for i in range(NX)]
                junk = T("junk", [128, DM], BF16, sp)
                ssq = [T("ssq%d" % i, [128, 1], F32, sp) for i in range(NX)]
                xn = [T("xn%d" % i, [128, DM], BF16, sp) for i in range(NX)]
                xnT = [T("xnT%d" % i, [128, 8, 512], BF16, sp) for i in range(2)]
                pst = [PS("pst%d" % i, [128, 8, 128], BF16, sp) for i in range(2)]
                NH = 4
                psh = [PS("psh%d" % i, [128, 512], F32, sp) for i in range(NH)]
                psn = [PS("psn0", [128, 512], F32, sp)] * NH
                psv = [PS("psv0", [128, 512], F32, sp)] * 2
                sq = [T("sq%d" % i, [128, 512], BF16, sp) for i in range(NH)]
                rs = [T("rs%d" % i, [128, 512], F32, sp) for i in range(NH)]
                NO = 4
                ot = [T("ot%d" % i, [128, 512], BF16, sp) for i in range(NO)]
                vt = [T("vt%d" % i, [128, 512], BF16, sp) for i in range(2)]
                dgt = [T("dgt%d" % i, [3, 512], F32, sp) for i in range(2)] * 2
                dgb = [T("dgb%d" % i, [3, 512], BF16, sp) for i in range(2)] * 2
                B = lambda n: [Buf() for _ in range(n)]
                b_xt, b_ss, b_xn, b_xnT, b_pst = B(NX), B(NX), B(NX), B(2), B(2)
                b_psh, b_psn, b_psv, b_sq, b_rs, b_ot, b_vt, b_dgt = B(NH), [Buf()] * NH, [Buf()] * 2, B(NH), B(NH), B(NO), B(2), B(2) * 2
                b_junk = Buf()
                cnt = {"h": 0, "o": 0, "v": 0, "x": 0}
                for i in range(2):
                    S.op("dve", lambda e: e.memset(vt[i][:], 1.0), writes=[b_vt[i]])

                def prepA(c):
                    for t in range(4):
                        tile = 4 * c + t
                        k = tile % NX
                        S.dma("sp", xt[k][:], xsrc[tile * 128:(tile + 1) * 128, :], reads=[xbuf], writes=[b_xt[k]], qbuf=b_xt[k])
                        S.op("act", lambda e: e.activation(out=junk[:], in_=xt[k][:], func=AF.Square, accum_out=ssq[k][:]),
                             reads=[b_xt[k]], writes=[b_junk, b_ss[k]])
                        S.op("act", lambda e: e.activation(out=ssq[k][:], in_=ssq[k][:], func=AF.Ln, bias=epsb[:, 0:1]),
                             reads=[b_ss[k], cbuf], writes=[b_ss[k]])
                        S.op("act", lambda e: e.activation(out=ssq[k][:], in_=ssq[k][:], func=AF.Exp, scale=-0.5),
                             reads=[b_ss[k]], writes=[b_ss[k]])
                        S.op("dve", lambda e: e.tensor_scalar(xn[k][:], xt[k][:], ssq[k][:, 0:1], None, ALU.mult),
                             reads=[b_xt[k], b_ss[k]], writes=[b_xn[k]])

                def prepB(c):
                    for t in range(4):
                        tile = 4 * c + t
                        k = tile % NX
                        k2 = tile % 2
                        for kc in range(8):
                            S.op("pe", lambda e: e.transpose(pst[k2][:, kc, :], xn[k][:, kc * 128:(kc + 1) * 128], ident[:, :]),
                                 reads=[b_xn[k], cbuf], writes=[b_pst[k2]])
                        S.op("dve", lambda e: e.tensor_tensor(
                            xnT[c % 2][:, :, t * 128:(t + 1) * 128], pst[k2][:, :, :],
                            nw32[:, :, None].to_broadcast([128, 8, 128]), ALU.mult),
                            reads=[b_pst[k2], bsm], writes=[b_xnT[c % 2]])

                def store(src_ap, dst_ap, bsrc, dname):
                    S.dma("pool", dst_ap, src_ap, reads=[bsrc], writes=[DB[dname]], qbuf=bsrc)

                def blocks(c):
                    cs = slice(c * 512, (c + 1) * 512)
                    xT = xnT[c % 2]
                    pendb = []

                    def part1(blk):
                        k = cnt["h"] % NH
                        cnt["h"] += 1
                        for kc in range(8):
                            S.op("pe", lambda e: e.matmul(psh[k][:, :], lhsT=w_bf[:, kc, blk * 128:(blk + 1) * 128], rhs=xT[:, kc, :],
                                                          start=(kc == 0), stop=(kc == 7)),
                                 reads=[b_wbf, b_xnT[c % 2]], writes=[b_psh[k]])
                        if blk <= 5:
                            S.op("act", lambda e: e.activation(out=sq[k][:], in_=psh[k][:], func=AF.Square),
                                 reads=[b_psh[k]], writes=[b_sq[k]])
                        return k

                    def part2(blk, k):
                        if blk == 9:
                            S.op("act", lambda e: e.activation(out=dgt[k][:], in_=psh[k][0:3, :], func=AF.Exp, scale=-1.0),
                                 reads=[b_psh[k]], writes=[b_dgt[k]])
                            S.op("dve", lambda e: e.tensor_scalar(dgt[k][:], dgt[k][:], 1.0, None, ALU.add),
                                 reads=[b_dgt[k]], writes=[b_dgt[k]])
                            S.op("dve", lambda e: e.reciprocal(dgt[k][:], dgt[k][:]), reads=[b_dgt[k]], writes=[b_dgt[k]])
                            S.op("dve", lambda e: e.tensor_copy(dgb[k][:], dgt[k][:]), reads=[b_dgt[k]], writes=[b_dgt[k]])
                            store(dgb[k][:], DG[:, cs], b_dgt[k], "DG")
                            return
                        o = cnt["o"] % NO
                        cnt["o"] += 1
                        if blk <= 5:
                            dh = 32 if blk == 2 else 64
                            bon = bones32 if blk == 2 else bones64
                            S.op("pe", lambda e: e.matmul(psn[k][:, :], lhsT=bon[:, :], rhs=sq[k][:], start=True, stop=True),
                                 reads=[b_sq[k], cbuf], writes=[b_psn[k]])
                            S.op("act", lambda e: e.activation(out=rs[k][:], in_=psn[k][:], func=AF.Ln,
                                                               bias=epsb[:, 2:3] if dh == 32 else epsb[:, 1:2]),
                                 reads=[b_psn[k], cbuf], writes=[b_rs[k]])
                            S.op("act", lambda e: e.activation(out=rs[k][:], in_=rs[k][:], func=AF.Exp, scale=-0.5),
                                 reads=[b_rs[k]], writes=[b_rs[k]])
                            S.op("dve", lambda e: e.scalar_tensor_tensor(ot[o][:], psh[k][:], gvs[:, blk:blk + 1], rs[k][:],
                                                                         ALU.mult, ALU.mult),
                                 reads=[b_psh[k], b_rs[k], bsm], writes=[b_ot[o]])
                            if blk == 0:
                                store(ot[o][0:64, :], QT[0, :, cs], b_ot[o], "QT")
                                store(ot[o][64:128, :], KT[0, :, cs], b_ot[o], "KT")
                            elif blk == 1:
                                store(ot[o][0:64, :], QT[1, :, cs], b_ot[o], "QT")
                                store(ot[o][64:128, :], KT[1, :, cs], b_ot[o], "KT")
                            elif blk == 2:
                                store(ot[o][0:32, :], QT[2, 0:32, cs], b_ot[o], "QT")
                                store(ot[o][32:64, :], QT[3, 32:64, cs], b_ot[o], "QT")
                                store(ot[o][64:128, :], KT[2, :, cs], b_ot[o], "KT")
                            elif blk in (3, 4):
                                store(ot[o][0:64, :], QT[4 + 2 * (blk - 3), :, cs], b_ot[o], "QT")
                                store(ot[o][64:128, :], QT[5 + 2 * (blk - 3), :, cs], b_ot[o], "QT")
                            else:
                                store(ot[o][0:64, :], KT[3, :, cs], b_ot[o], "KT")
                                store(ot[o][64:128, :], KT[4, :, cs], b_ot[o], "KT")
                        elif blk == 6:
                            S.op("act", lambda e: e.activation(out=ot[o][:], in_=psh[k][:], func=AF.Copy),
                                 reads=[b_psh[k]], writes=[b_ot[o]])
                            store(ot[o][0:64, :], KT[5, :, cs], b_ot[o], "KT")
                            store(ot[o][64:128, :], KT[6, :, cs], b_ot[o], "KT")
                        else:
                            S.op("act", lambda e: e.activation(out=rs[k][:], in_=psh[k][:], func=AF.Exp, scale=-1.0),
                                 reads=[b_psh[k]], writes=[b_rs[k]])
                            S.op("dve", lambda e: e.tensor_scalar(rs[k][:], rs[k][:], 1.0, None, ALU.add),
                                 reads=[b_rs[k]], writes=[b_rs[k]])
                            S.op("dve", lambda e: e.reciprocal(rs[k][:], rs[k][:]), reads=[b_rs[k]], writes=[b_rs[k]])
                            S.op("dve", lambda e: e.tensor_tensor(ot[o][:], psh[k][:], rs[k][:], ALU.mult),
                                 reads=[b_psh[k], b_rs[k]], writes=[b_ot[o]])
                            store(ot[o][0:64, :], GT[2 * (blk - 7), :, cs], b_ot[o], "GT")
                            store(ot[o][64:128, :], GT[2 * (blk - 7) + 1, :, cs], b_ot[o], "GT")

                    pq = []
                    for blk in range(10):
                        k = part1(blk)
                        pq.append((blk, k))
                        if len(pq) > 1:
                            part2(*pq.pop(0))
                    while pq:
                        part2(*pq.pop(0))

                def vblocks(c):
                    xT = xnT[c % 2]
                    for t in range(4):
                        k = cnt["v"] % 2
                        cnt["v"] += 1
                        for kc in range(8):
                            S.op("pe", lambda e: e.matmul(psv[k][:, 0:320], lhsT=xT[:, kc, t * 128:(t + 1) * 128], rhs=w_bf[:, kc, 1280:1600],
                                                          start=(kc == 0), stop=(kc == 7)),
                                 reads=[b_wbf, b_xnT[c % 2]], writes=[b_psv[k]])
                        S.op("dve", lambda e: e.tensor_copy(vt[k][:, 0:64], psv[k][:, 0:64]), reads=[b_psv[k]], writes=[b_vt[k]])
                        S.op("dve", lambda e: e.tensor_copy(vt[k][:, 128:256], psv[k][:, 64:192]), reads=[b_psv[k]], writes=[b_vt[k]])
                        S.op("dve", lambda e: e.tensor_copy(vt[k][:, 320:448], psv[k][:, 192:320]), reads=[b_psv[k]], writes=[b_vt[k]])
                        store(vt[k][:], VV[:, 4 * c + t, :], b_vt[k], "VV")

                nchunks = NCH if STAGE > 1 else 2
                prepA(0)
                prepB(0)
                for c in range(nchunks):
                    if c + 1 < nchunks:
                        prepA(c + 1)
                    blocks(c)
                    if c + 1 < nchunks:
                        prepB(c + 1)
                    vblocks(c)
                S.barrier()


        kcn = T("kcn", [64, 512], BF16)
        vcx = T("vcx", [128, 4, 64], BF16)
        b_kcn, b_vcx = Buf("kcn"), Buf("vcx")

        def compress_phase(L):
            with ExitStack() as sc:
                xT = T("c_xT", [64, SEQ], BF16, sc)
                w1s = [T("c_w1s%d" % i, [64, 8, 256], F32, sc) for i in range(2)]
                w1b = T("c_w1b", [64, 32, 256], BF16, sc)
                w2s = T("c_w2s", [128, 2, 64], F32, sc)
                w2b = T("c_w2b", [128, 2, 64], BF16, sc)
                pTs = T("c_pTs", [64, 32], F32, sc)
                pTb = T("c_pTb", [64, 32], BF16, sc)
                b1 = T("c_b1", [128, 2], F32, sc)
                b2 = T("c_b2", [64, 1], F32, sc)
                kg = T("c_kg", [64, 1], F32, sc)
                u = T("c_u", [128, 512], F32, sc)
                u2 = T("c_u2", [128, 512], F32, sc)
                gl = T("c_gl", [128, 2, 512], BF16, sc)
                kc = T("c_kc", [64, 512], F32, sc)
                sqc = T("c_sq", [64, 512], BF16, sc)
                rsc = T("c_rs", [64, 512], F32, sc)
                vcT = T("c_vcT", [64, 512], BF16, sc)
                ps_h = [PS("c_psh%d" % i, [128, 512], F32, sc) for i in range(2)]
                ps_pb = PS("c_pspb", [128, 512], F32, sc)
                ps_c = PS("c_psc", [128, 512], F32, sc)
                ps_t = PS("c_pst", [128, 4, 64], BF16, sc)
                bx, bw1s, bw, bu, bph, bpb, bpc, bpt, bkc = Buf(), [Buf(), Buf()], Buf(), Buf(), [Buf(), Buf()], Buf(), Buf(), Buf(), Buf()
                S.dma("sp", kg[:], I["kcgT"][L], writes=[bw], qbuf=bw)
                S.op("dve", lambda e: e.tensor_scalar(kg[:], kg[:], 8.0, None, ALU.mult), reads=[bw], writes=[bw])
                S.op("dve", lambda e: e.memset(kcn[:], 0.0), writes=[b_kcn])
                S.op("dve", lambda e: e.memset(vcT[:], 0.0), writes=[bkc])
                for kv in range(2):
                    S.dma("sp", xT[:], KT[5 + kv], reads=[DB["KT"]], writes=[bx], qbuf=bx)
                    for q4 in range(4):
                        k = q4 % 2
                        S.dma("sp", w1s[k][:], I["w1"][L, kv, q4 * 512:(q4 + 1) * 512, :].rearrange("(t d) h -> d t h", d=64),
                              writes=[bw1s[k]], qbuf=bw1s[k])
                        S.op("dve", lambda e: e.tensor_copy(w1b[:, q4 * 8:(q4 + 1) * 8, :], w1s[k][:]), reads=[bw1s[k]], writes=[bw])
                    S.dma("sp", w2s[:], I["w2"][L, kv].rearrange("(c p) d -> p c d", p=128), writes=[bw], qbuf=bw)
                    S.op("dve", lambda e: e.tensor_copy(w2b[:], w2s[:]), reads=[bw], writes=[bw])
                    S.dma("sp", pTs[:], I["posT"][L, kv], writes=[bw], qbuf=bw)
                    S.op("dve", lambda e: e.tensor_copy(pTb[:], pTs[:]), reads=[bw], writes=[bw])
                    S.dma("sp", b1[:], I["b1T"][L, kv], writes=[bw], qbuf=bw)
                    S.dma("sp", b2[:], I["b2T"][L, kv], writes=[bw], qbuf=bw)
                    for hc in range(2):
                        for t in range(32):
                            S.op("pe", lambda e: e.matmul(ps_pb[:, hc:hc + 1], lhsT=w1b[:, t, hc * 128:(hc + 1) * 128], rhs=pTb[:, t:t + 1],
                                                          start=(t == 0), stop=(t == 31)), reads=[bw], writes=[bpb])
                    S.op("dve", lambda e: e.tensor_tensor(b1[:], b1[:], ps_pb[:, 0:2], ALU.add), reads=[bpb, bw], writes=[bw])
                    for hc in range(2):
                        for t in range(32):
                            S.op("pe", lambda e: e.matmul(ps_h[hc][:, 0:511], lhsT=w1b[:, t, hc * 128:(hc + 1) * 128],
                                                          rhs=xT[:, t:t + 16 * 510 + 1:16], start=(t == 0), stop=(t == 31)),
                                 reads=[bw, bx], writes=[bph[hc]])
                        S.op("dve", lambda e: e.tensor_scalar(u[:, 0:511], ps_h[hc][:, 0:511], b1[:, hc:hc + 1], None, ALU.add),
                             reads=[bph[hc], bw], writes=[bu])
                        S.op("dve", lambda e: e.tensor_tensor(u2[:, 0:511], u[:, 0:511], u[:, 0:511], ALU.mult), reads=[bu], writes=[bu])
                        S.op("dve", lambda e: e.tensor_scalar(u2[:, 0:511], u2[:, 0:511], 0.044715, 1.0, ALU.mult, ALU.add), reads=[bu], writes=[bu])
                        S.op("dve", lambda e: e.tensor_tensor(u2[:, 0:511], u2[:, 0:511], u[:, 0:511], ALU.mult), reads=[bu], writes=[bu])
                        S.op("act", lambda e: e.activation(out=u2[:, 0:511], in_=u2[:, 0:511], func=AF.Exp, scale=-1.5957691216057308),
                             reads=[bu], writes=[bu])
                        S.op("dve", lambda e: e.tensor_scalar(u2[:, 0:511], u2[:, 0:511], 1.0, None, ALU.add), reads=[bu], writes=[bu])
                        S.op("dve", lambda e: e.reciprocal(u2[:, 0:511], u2[:, 0:511]), reads=[bu], writes=[bu])
                        S.op("dve", lambda e: e.tensor_tensor(gl[:, hc, 0:511], u[:, 0:511], u2[:, 0:511], ALU.mult), reads=[bu], writes=[bu])
                    for hc in range(2):
                        S.op("pe", lambda e: e.matmul(ps_c[0:64, 0:511], lhsT=w2b[:, hc, :], rhs=gl[:, hc, 0:511],
                                                      start=(hc == 0), stop=(hc == 1)), reads=[bw, bu], writes=[bpc])
                    if kv == 0:
                        S.op("dve", lambda e: e.tensor_scalar(kc[:, 0:511], ps_c[0:64, 0:511], b2[:, 0:1], None, ALU.add),
                             reads=[bpc, bw], writes=[bkc])
                        S.op("act", lambda e: e.activation(out=sqc[:, 0:511], in_=kc[:, 0:511], func=AF.Square), reads=[bkc], writes=[bkc])
                        S.op("pe", lambda e: e.matmul(ps_c[0:64, 0:511], lhsT=bones64[0:64, 0:64], rhs=sqc[:, 0:511], start=True, stop=True),
                             reads=[bkc, cbuf], writes=[bpc])
                        S.op("act", lambda e: e.activation(out=rsc[:, 0:511], in_=ps_c[0:64, 0:511], func=AF.Ln, bias=epsb[0:64, 1:2]),
                             reads=[bpc, cbuf], writes=[bkc])
                        S.op("act", lambda e: e.activation(out=rsc[:, 0:511], in_=rsc[:, 0:511], func=AF.Exp, scale=-0.5), reads=[bkc], writes=[bkc])
                        S.op("dve", lambda e: e.scalar_tensor_tensor(kcn[:, 0:511], kc[:, 0:511], kg[:, 0:1], rsc[:, 0:511], ALU.mult, ALU.mult),
                             reads=[bkc, bw], writes=[b_kcn])
                    else:
                        S.op("dve", lambda e: e.tensor_scalar(vcT[:, 0:511], ps_c[0:64, 0:511], b2[:, 0:1], None, ALU.add),
                             reads=[bpc, bw], writes=[bkc])
                        for cb in range(4):
                            S.op("pe", lambda e: e.transpose(ps_t[:, cb, :], vcT[:, cb * 128:(cb + 1) * 128], ident[0:64, 0:64]),
                                 reads=[bkc, cbuf], writes=[bpt])
                        S.op("dve", lambda e: e.tensor_copy(vcx[:], ps_t[:]), reads=[bpt], writes=[b_vcx])
                S.barrier()

        cq = S.new_queue("c")

        def attn_phase(L):
            lam_init = 0.8 - 0.6 * math.exp(-0.3 * L)
            with ExitStack() as sa:
                Vall = T("Vall", [128, 64, 512], BF16, sa)
                KTs = T("KTs", [128, SEQ], BF16, sa)
                Rm = T("Rm_sb", [128, SEQ], BF16, sa)
                ovl = T("ovl_sb", [128, 4, 128], BF16, sa)
                bV, bK, bR = Buf("Vall"), [Buf("K0"), Buf("K1")], Buf("R")
                S.dma("sp", Vall[:], VV[:], reads=[DB["VV"]], writes=[bV], qbuf=bV)
                S.dma("sp", KTs[0:64, :], KT[0], reads=[DB["KT"]], writes=[bK[0]], qbuf=bK[0])
                S.dma("sp", KTs[64:128, :], KT[1], reads=[DB["KT"]], writes=[bK[1]], qbuf=bK[1])
                S.dma("sp", Rm[:], I["Rm"][:], writes=[bR], qbuf=bR)
                S.dma("sp", ovl[:], I["ovl"][:], writes=[bR], qbuf=bR)
                sm = Buf("attn_small")
                esink = T("esink", [128, 1], F32, sa)
                S.dma("sp", esink[:], dram_ap(I["sink"], L, [[0, 128], [1, 1]]), writes=[sm], qbuf=sm)
                S.op("act", lambda e: e.activation(out=esink[:], in_=esink[:], func=AF.Exp), reads=[sm], writes=[sm])
                dl = T("dl", [128, 4, 32], F32, sa)
                S.dma("sp", dl[:], dram_ap(I["dlam"], L * 128, [[0, 128], [32, 4], [1, 32]]), writes=[sm], qbuf=sm)
                pr = T("dlp", [128, 2, 32], F32, sa)
                s12 = T("s12", [128, 2], F32, sa)
                neglam = T("neglam", [128, 1], F32, sa)
                S.op("dve", lambda e: e.tensor_tensor(pr[:, 0, :], dl[:, 0, :], dl[:, 1, :], ALU.mult), reads=[sm], writes=[sm])
                S.op("dve", lambda e: e.tensor_tensor(pr[:, 1, :], dl[:, 2, :], dl[:, 3, :], ALU.mult), reads=[sm], writes=[sm])
                S.op("dve", lambda e: e.reduce_sum(out=s12[:], in_=pr[:], axis=AX.X), reads=[sm], writes=[sm])
                S.op("act", lambda e: e.activation(out=s12[:], in_=s12[:], func=AF.Exp), reads=[sm], writes=[sm])
                S.op("dve", lambda e: e.tensor_tensor(neglam[:], s12[:, 1:2], s12[:, 0:1], ALU.subtract), reads=[sm], writes=[sm])
                S.op("dve", lambda e: e.tensor_scalar(neglam[:], neglam[:], -lam_init, None, ALU.add), reads=[sm], writes=[sm])
                sgC = T("sgC", [64, 1], F32, sa)
                S.dma("sp", sgC[:], I["sublnT"][L], writes=[sm], qbuf=sm)
                S.op("dve", lambda e: e.tensor_scalar(sgC[:], sgC[:], 8.0 * (1.0 - lam_init), None, ALU.mult), reads=[sm], writes=[sm])

                NP_ = 5
                pt = [T("pt%d" % i, [128, 512], BF16, sa) for i in range(NP_)]
                b_pt = [Buf() for _ in range(NP_)]
                ps_s = [PS("ps_s%d" % i, [128, 512], F32, sa) for i in range(4)]
                ps_o = [PS("ps_o%d" % i, [128, 512], F32, sa) for i in range(2)]
                ps_x = PS("ps_x", [128, 512], F32, sa)
                ps_imp = PS("ps_imp", [128, 4, 128], F32, sa)
                b_ps_s, b_ps_o = [Buf(), Buf(), Buf(), Buf()], [Buf(), Buf()]
                b_ps_x, b_ps_imp = Buf(), Buf()
                ps_tr, b_ps_tr = ps_imp, b_ps_imp
                ps_oc, b_ps_oc = ps_x, b_ps_x
                NQ = 3
                qt = [T("qt%d" % i, [128, 2, 512], BF16, sa) for i in range(NQ)]
                gtl = [T("gtl%d" % i, [64, 512], BF16, sa) for i in range(NQ)]
                b_qt = [Buf() for _ in range(NQ)]
                rd = [T("rd%d" % i, [128, 512], F32, sa) for i in range(2)]
                fo = [T("fo%d" % i, [64, 512], F32, sa) for i in range(3)]
                yo = [T("yo%d" % i, [64, 512], BF16, sa) for i in range(2)]
                sqa = T("sqa", [64, 512], BF16, sa)
                b_fin = Buf("fin")
                b_yo = [Buf(), Buf()]
                cn = {"s": 0, "p": 0, "q": 0, "y": 0, "dve": 0}

                pend = []
                DEPTH_LA = 3

                def flush():
                    while pend:
                        pend.pop(0)()

                def tile_step(kt_ap, q_ap, bias_ap, strip_ap, vx_ap, pso, bpso, first, last, kbuf, qbuf_, extra=None, after=None, mask_eng=None):
                    k = cn["s"] % 4
                    cn["s"] += 1
                    p = cn["p"] % NP_
                    cn["p"] += 1
                    scale = tile_step.scale
                    S.op("pe", lambda e: e.matmul(ps_s[k][:, :], lhsT=kt_ap, rhs=q_ap, start=True, stop=(extra is None)),
                         reads=[kbuf, qbuf_], writes=[b_ps_s[k]])
                    if extra is not None:
                        S.op("pe", lambda e: e.matmul(ps_s[k][:, :], lhsT=extra[0], rhs=extra[1], start=False, stop=True),
                             reads=[bR, extra[2]], writes=[b_ps_s[k]])
                    if bias_ap is None:
                        S.op("act", lambda e: e.activation(out=pt[p][:], in_=ps_s[k][:], func=AF.Exp, scale=scale),
                             reads=[b_ps_s[k]], writes=[b_pt[p]])
                    else:
                        S.op("act", lambda e: e.activation(out=pt[p][:], in_=ps_s[k][:], func=AF.Exp, scale=scale, bias=bias_ap),
                             reads=[b_ps_s[k], cbuf], writes=[b_pt[p]])
                    if strip_ap is not None:
                        cn["dve"] += 1
                        eng = "dve"
                        S.op(eng, lambda e: e.tensor_tensor(pt[p][:], pt[p][:], strip_ap, ALU.mult),
                             reads=[b_pt[p], b_strips], writes=[b_pt[p]])

                    def pv_part():
                        S.op("pe", lambda e: e.matmul(pso[:, :], lhsT=vx_ap, rhs=pt[p][:], start=first, stop=last),
                             reads=[bV, b_pt[p]], writes=[bpso])
                        if after is not None:
                            after()
                    pend.append(pv_part)
                    if len(pend) > DEPTH_LA:
                        pend.pop(0)()

                def store_y(c, m, acc_ap, gate_ap, q_b):
                    y = cn["y"] % 2
                    cn["y"] += 1
                    S.op("dve", lambda e: e.tensor_tensor(yo[y][:], acc_ap, gate_ap, ALU.mult), reads=[b_fin, q_b], writes=[b_yo[y]])
                    S.dma("pool", YTo[c, m * 64:(m + 1) * 64, :], yo[y][:], reads=[b_yo[y]], writes=[DB["YTo"]], qbuf=b_yo[y])

                def load_q(c, qidx_list, pbs, m):
                    i = cn["q"] % NQ
                    cn["q"] += 1
                    cs = slice(c * 512, (c + 1) * 512)
                    for j, (qi, pb) in enumerate(zip(qidx_list, pbs)):
                        S.dma("sp", qt[i][pb:pb + 64, j, :], QT[qi, :, cs], reads=[DB["QT"]], writes=[b_qt[i]], qbuf=b_qt[i])
                    S.dma("sp", gtl[i][:], GT[m, :, cs], reads=[DB["GT"]], writes=[b_qt[i]], qbuf=b_qt[i])
                    return i

                nch = NCH if STAGE > 3 else 2

                tile_step.scale = 0.125
                for c in range(nch):
                    i = load_q(c, [0], [0], 0)
                    kbs = list(range(max(0, 4 * c - 16), 4 * c + 4))

                    def finA(c=c, i=i):
                        po = ps_o[c % 2]
                        S.op("dve", lambda e: e.reciprocal(rd[0][64:128, :], po[64:128, :]), reads=[b_ps_o[c % 2]], writes=[b_fin])
                        S.op("dve", lambda e: e.tensor_tensor(fo[0][:], po[0:64, :], rd[0][64:128, :], ALU.mult), reads=[b_ps_o[c % 2], b_fin], writes=[b_fin])
                        store_y(c, 0, fo[0][:], gtl[i][:], b_qt[i])
                    for n, kb in enumerate(kbs):
                        dlt = 512 * c - 128 * kb
                        tile_step(KTs[0:64, kb * 128:(kb + 1) * 128], qt[i][0:64, 0, :], None,
                                  strips[0][:, dlt + 384:dlt + 896], Vall[:, kb, 0:128], ps_o[c % 2], b_ps_o[c % 2],
                                  n == 0, n == len(kbs) - 1, bK[0], b_qt[i], after=finA if n == len(kbs) - 1 else None)
                flush()

                if STAGE > 2:
                    S.dma("sp", KTs[0:64, :], KT[2], reads=[DB["KT"]], writes=[bK[0]], qbuf=bK[0])
                for c in range(nch):
                    i = load_q(c, [1], [64], 1)
                    kbs = list(range(max(0, 4 * c - 1), 4 * c + 4))

                    def finB(c=c, i=i):
                        po = ps_o[c % 2]
                        S.op("dve", lambda e: e.tensor_scalar(rd[0][0:64, :], po[0:64, :], esink[0:64, 0:1], None, ALU.add),
                             reads=[b_ps_o[c % 2], sm], writes=[b_fin])
                        S.op("dve", lambda e: e.reciprocal(rd[0][0:64, :], rd[0][0:64, :]), reads=[b_fin], writes=[b_fin])
                        S.op("dve", lambda e: e.tensor_tensor(fo[0][:], po[64:128, :], rd[0][0:64, :], ALU.mult), reads=[b_ps_o[c % 2], b_fin], writes=[b_fin])
                        store_y(c, 1, fo[0][:], gtl[i][:], b_qt[i])
                    for n, kb in enumerate(kbs):
                        dlt = 512 * c - 128 * kb
                        tile_step(KTs[64:128, kb * 128:(kb + 1) * 128], qt[i][64:128, 0, :], None,
                                  strips[1][:, dlt + 384:dlt + 896], Vall[:, kb, 64:192], ps_o[c % 2], b_ps_o[c % 2],
                                  n == 0, n == len(kbs) - 1, bK[1], b_qt[i], after=finB if n == len(kbs) - 1 else None)
                flush()

                if STAGE > 2:
                    S.dma("sp", KTs[64:128, :], KT[4], reads=[DB["KT"]], writes=[bK[1]], qbuf=bK[1])
                    tile_step.scale = 32.0 ** -0.5
                    for c in range(nch):
                        i = load_q(c, [2, 3], [0, 0], 2)
                        kbs = list(range(0, 4 * c + 4))

                        def finC(c=c, i=i):
                            S.op("dve", lambda e: e.reciprocal(rd[0][64:128, :], ps_o[0][64:128, :]), reads=[b_ps_o[0]], writes=[b_fin])
                            S.op("dve", lambda e: e.reciprocal(rd[1][64:128, :], ps_o[1][64:128, :]), reads=[b_ps_o[1]], writes=[b_fin])
                            S.op("dve", lambda e: e.tensor_tensor(fo[0][:], ps_o[0][0:64, :], rd[0][64:128, :], ALU.mult), reads=[b_ps_o[0], b_fin], writes=[b_fin])
                            S.op("dve", lambda e: e.tensor_tensor(fo[1][:], ps_o[1][0:64, :], rd[1][64:128, :], ALU.mult), reads=[b_ps_o[1], b_fin], writes=[b_fin])
                            S.op("dve", lambda e: e.scalar_tensor_tensor(fo[0][:], fo[1][:], neglam[0:64, 0:1], fo[0][:], ALU.mult, ALU.add),
                                 reads=[b_fin, sm], writes=[b_fin])
                            S.op("act", lambda e: e.activation(out=sqa[:], in_=fo[0][:], func=AF.Square), reads=[b_fin], writes=[b_fin])
                            S.op("pe", lambda e: e.matmul(ps_x[0:64, :], lhsT=bones64[0:64, 0:64], rhs=sqa[:], start=True, stop=True),
                                 reads=[b_fin, cbuf], writes=[b_ps_x])
                            S.op("act", lambda e: e.activation(out=fo[1][:], in_=ps_x[0:64, :], func=AF.Ln, bias=epsb[0:64, 1:2]),
                                 reads=[b_ps_x, cbuf], writes=[b_fin])
                            S.op("act", lambda e: e.activation(out=fo[1][:], in_=fo[1][:], func=AF.Exp, scale=-0.5), reads=[b_fin], writes=[b_fin])
                            S.op("dve", lambda e: e.scalar_tensor_tensor(fo[0][:], fo[0][:], sgC[:, 0:1], fo[1][:], ALU.mult, ALU.mult),
                                 reads=[b_fin, sm], writes=[b_fin])
                            store_y(c, 2, fo[0][:], gtl[i][:], b_qt[i])
                        for n, kb in enumerate(kbs):
                            dlt = 512 * c - 128 * kb
                            near = dlt <= 1536
                            for mp in range(2):
                                tile_step(KTs[32 * mp:32 * mp + 32, kb * 128:(kb + 1) * 128], qt[i][32 * mp:32 * mp + 32, mp, :],
                                          None if near else b31[:, 2:3],
                                          strips[2][:, dlt + 384:dlt + 896] if near else None,
                                          Vall[:, kb, 192:320], ps_o[mp], b_ps_o[mp], n == 0, n == len(kbs) - 1, bK[0], b_qt[i],
                                          after=finC if (n == len(kbs) - 1 and mp == 1) else None)
                    flush()

                if STAGE > 2:
                    S.dma("sp", KTs[0:64, :], KT[3], reads=[DB["KT"]], writes=[bK[0]], qbuf=bK[0])
                    tile_step.scale = 0.125
                    qd = [T("qd%d" % i, [64, 4, 512], BF16, sa) for i in range(2)]
                    gb = [T("gb%d" % i, [64, 3, 512], BF16, sa) for i in range(2)]
                    am = [T("am0", [128, 4, 128], F32, sa)] * 2
                    b_am = Buf()
                    b_qd = [Buf(), Buf()]
                    eb = [T("eb%d" % i, [128, 4, 512], BF16, sa) for i in range(2)]
                    b_eb = [Buf(), Buf()]
                    im = T("im", [128, 4, 128], F32, sa)
                    wk = T("wk", [128, 128], F32, sa)
                    m8 = T("m8", [128, 16], F32, sa)
                    nsb = im
                    nselT = [T("nselT%d" % i, [128, 512], BF16, sa) for i in range(2)]
                    fo2 = [fo[2], T("fo3", [64, 512], F32, sa)]
                    b_tk, b_nsel, b_fo2 = Buf(), [Buf(), Buf()], [Buf(), Buf()]
                    info = {}

                    def loadsD(c):
                        cs = slice(c * 512, (c + 1) * 512)
                        d = c % 2
                        for hd in range(4):
                            S.dma("sp", qd[d][:, hd, :], QT[4 + hd, :, cs], reads=[DB["QT"]], writes=[b_qd[d]], qbuf=b_qd[d])
                        for br in range(3):
                            S.dma("sp", gb[d][:, br, :], dram_ap(DG, br * SEQ + c * 512, [[0, 64], [1, 512]]),
                                  reads=[DB["DG"]], writes=[b_qd[d]], qbuf=b_qd[d])
                        S.dma("sp", am[d][:], I["addm"][:, 4 * c:4 * c + 4, :], writes=[b_am], qbuf=b_am)
                        i = load_q(c, [4], [64], 3)
                        info[c] = (d, i)

                    def cmp_topk(c):
                        d, i = info[c]
                        ncb = c // 4 + 1
                        for hd in range(4):
                            ee = eb[hd % 2]
                            bee = b_eb[hd % 2]
                            for cb in range(ncb):
                                m = c - 4 * cb
                                far = m >= 7
                                k = cn["s"] % 4
                                cn["s"] += 1
                                S.op("pe", lambda e: e.matmul(ps_s[k][:, :], lhsT=kcn[:, cb * 128:(cb + 1) * 128], rhs=qd[d][:, hd, :],
                                                              start=True, stop=True), reads=[b_kcn, b_qd[d]], writes=[b_ps_s[k]])
                                if far:
                                    S.op("act", lambda e: e.activation(out=ee[:, cb, :], in_=ps_s[k][:], func=AF.Exp, scale=0.125,
                                                                       bias=b31[:, 3 + hd:4 + hd]), reads=[b_ps_s[k], cbuf], writes=[bee])
                                else:
                                    S.op("act", lambda e: e.activation(out=ee[:, cb, :], in_=ps_s[k][:], func=AF.Exp, scale=0.125),
                                         reads=[b_ps_s[k]], writes=[bee])
                                    S.op("dve", lambda e: e.tensor_tensor(ee[:, cb, :], ee[:, cb, :], strips[5 + hd][:, 512 * m:512 * m + 512], ALU.mult),
                                         reads=[bee, b_strips], writes=[bee])
                            yield
                            for cb in range(ncb):
                                S.op("pe", lambda e: e.matmul(ps_x[:, :], lhsT=ones128[:, :], rhs=ee[:, cb, :], start=(cb == 0), stop=(cb == ncb - 1)),
                                     reads=[bee, cbuf], writes=[b_ps_x])
                            S.op("dve", lambda e: e.tensor_scalar(rd[1][:], ps_x[:], 1e-30, None, ALU.max), reads=[b_ps_x], writes=[b_fin])
                            S.op("dve", lambda e: e.reciprocal(rd[1][:], rd[1][:]), reads=[b_fin], writes=[b_fin])
                            yield
                            for cb in range(ncb):
                                S.op("dve", lambda e: e.tensor_tensor(ee[:, cb, :], ee[:, cb, :], rd[1][:], ALU.mult), reads=[bee, b_fin], writes=[bee])
                            yield
                            if hd == 0:
                                for cb in range(ncb):
                                    S.op("pe", lambda e: e.matmul(ps_oc[0:64, :], lhsT=vcx[:, cb, :], rhs=ee[:, cb, :], start=(cb == 0), stop=(cb == ncb - 1)),
                                         reads=[bee, b_vcx], writes=[b_ps_oc])
                                S.op("dve", lambda e: e.tensor_tensor(fo2[d][:], ps_oc[0:64, :], gb[d][:, 0, :], ALU.mult),
                                     reads=[b_ps_oc, b_qd[d]], writes=[b_fo2[d]])
                            for t in range(4):
                                for cb in range(ncb):
                                    S.op("pe", lambda e: e.matmul(ps_imp[:, t, :], lhsT=ee[:, cb, t * 128:(t + 1) * 128], rhs=ovl[:, cb, :],
                                                                  start=(hd == 0 and cb == 0), stop=(hd == 3 and cb == ncb - 1)),
                                         reads=[bee, bR], writes=[b_ps_imp])
                            yield
                        for t in range(4):
                            S.op("dve", lambda e: e.tensor_tensor(im[:, t, :], ps_imp[:, t, :], am[d][:, t, :], ALU.add), reads=[b_ps_imp, b_am], writes=[b_tk])
                        yield
                        for t in range(4):
                            S.op("dve", lambda e: e.max(out=m8[:, 0:8], in_=im[:, t, :]), reads=[b_tk], writes=[b_tk])
                            S.op("dve", lambda e: e.match_replace(out=wk[:], in_to_replace=m8[:, 0:8], in_values=im[:, t, :], imm_value=-3.0e38),
                                 reads=[b_tk], writes=[b_tk])
                            S.op("dve", lambda e: e.max(out=m8[:, 8:16], in_=wk[:]), reads=[b_tk], writes=[b_tk])
                            S.op("dve", lambda e: e.tensor_scalar(nsb[:, t, :], im[:, t, :], m8[:, 15:16], -1024.0, ALU.is_lt, ALU.mult), reads=[b_tk], writes=[b_tk])
                            S.op("pe", lambda e: e.transpose(ps_tr[:, t, :], nsb[:, t, :], identf[:, :]), reads=[b_tk, cbuf], writes=[b_ps_tr])
                            yield
                        for t in range(4):
                            S.op("dve", lambda e: e.tensor_copy(nselT[d][:, t * 128:(t + 1) * 128], ps_tr[:, t, :]), reads=[b_ps_tr], writes=[b_nsel[d]])
                        yield

                    def step(bg):
                        if bg[0] is not None:
                            try:
                                next(bg[0])
                            except StopIteration:
                                bg[0] = None

                    loadsD(0)
                    for _ in cmp_topk(0):
                        pass
                    for c in range(nch):
                        d, i = info[c]
                        bg = [None]
                        if c + 1 < nch:
                            loadsD(c + 1)
                            bg[0] = cmp_topk(c + 1)

                        def finW(d=d):
                            S.op("dve", lambda e: e.reciprocal(rd[0][64:128, :], ps_o[1][64:128, :]), reads=[b_ps_o[1]], writes=[b_fin])
                            S.op("dve", lambda e: e.tensor_tensor(fo[0][:], ps_o[1][0:64, :], rd[0][64:128, :], ALU.mult), reads=[b_ps_o[1], b_fin], writes=[b_fin])
                            S.op("dve", lambda e: e.tensor_tensor(fo[0][:], fo[0][:], gb[d][:, 2, :], ALU.mult), reads=[b_fin, b_qd[d]], writes=[b_fin])
                            S.op("dve", lambda e: e.tensor_tensor(fo2[d][:], fo2[d][:], fo[0][:], ALU.add), reads=[b_fin, b_fo2[d]], writes=[b_fo2[d]])
                        kbs = list(range(max(0, 4 * c - 4), 4 * c + 4))
                        for n, kb in enumerate(kbs):
                            dlt = 512 * c - 128 * kb
                            tile_step(KTs[64:128, kb * 128:(kb + 1) * 128], qt[i][64:128, 0, :], None,
                                      strips[4][:, dlt + 384:dlt + 896], Vall[:, kb, 384:512], ps_o[1], b_ps_o[1],
                                      n == 0, n == len(kbs) - 1, bK[1], b_qt[i], after=finW if n == len(kbs) - 1 else None)
                            step(bg)

                        def finS(c=c, d=d, i=i):
                            S.op("dve", lambda e: e.reciprocal(rd[0][0:64, :], ps_o[0][0:64, :]), reads=[b_ps_o[0]], writes=[b_fin])
                            S.op("dve", lambda e: e.tensor_tensor(fo[0][:], ps_o[0][64:128, :], rd[0][0:64, :], ALU.mult), reads=[b_ps_o[0], b_fin], writes=[b_fin])
                            S.op("dve", lambda e: e.tensor_tensor(fo[0][:], fo[0][:], gb[d][:, 1, :], ALU.mult), reads=[b_fin, b_qd[d]], writes=[b_fin])
                            S.op("dve", lambda e: e.tensor_tensor(fo2[d][:], fo2[d][:], fo[0][:], ALU.add), reads=[b_fin, b_fo2[d]], writes=[b_fo2[d]])
                            y = cn["y"] % 2
                            cn["y"] += 1
                            S.op("dve", lambda e: e.tensor_tensor(yo[y][:], fo2[d][:], gtl[i][:], ALU.mult), reads=[b_fo2[d], b_qt[i]], writes=[b_yo[y]])
                            S.dma("pool", YTo[c, 192:256, :], yo[y][:], reads=[b_yo[y]], writes=[DB["YTo"]], qbuf=b_yo[y])
                        kbs = list(range(0, 4 * c + 4))
                        for n, kb in enumerate(kbs):
                            dlt = 512 * c - 128 * kb
                            near = dlt <= 1536
                            tile_step(KTs[0:64, kb * 128:(kb + 1) * 128], qd[d][:, 0, :], None if near else b31[:, 3:4],
                                      strips[3][:, dlt + 384:dlt + 896] if near else None, Vall[:, kb, 256:384], ps_o[0], b_ps_o[0],
                                      n == 0, n == len(kbs) - 1, bK[0], b_qd[d], extra=(Rm[:, kb * 128:(kb + 1) * 128], nselT[d][:], b_nsel[d]),
                                      after=finS if n == len(kbs) - 1 else None)
                            step(bg)
                        while bg[0] is not None:
                            step(bg)
                        flush()
                        if mode == "fused":
                            S._wait("pool", S._deps([DB["YTo"]], [DB["YTa"]]))
                            ins = nc.gpsimd.collective_compute("AllGather", op=ALU.bypass, replica_groups=[[0, 1, 2, 3], [4, 5, 6, 7]],
                                                               ins=[YTo[c]], outs=[YTa[c]])
                            S.cnt[cq] += 1
                            ins.then_inc(S.sem[cq], 1)
                            S.ninst += 1
                            S._mark(cq, S.cnt[cq], [DB["YTo"]], [DB["YTa"]])
                S.barrier()

        def outproj_phase(L, xsrc, xbuf, dst, dname, gather=True, nchk=NCH):
            with ExitStack() as so:
                wo = T("wo", [128, 8, DM], BF16, so)
                wos = [T("wos%d" % i, [128, DM], F32, so) for i in range(2)]
                bwos, bwo = [Buf(), Buf()], Buf()
                for kc in range(8):
                    k = kc % 2
                    S.dma("sp", wos[k][:], I["wosel"][L, kc * 128:(kc + 1) * 128, :], writes=[bwos[k]], qbuf=bwos[k])
                    S.op("pool" if k else "dve", lambda e: e.tensor_copy(wo[:, kc, :], wos[k][:]), reads=[bwos[k]], writes=[bwo])
                yT = [T("yT%d" % i, [128, 8, 512], BF16, so) for i in range(2)]
                xr = [T("xr%d" % i, [128, DM], F32, so) for i in range(3)]
                psq = [PS("psq%d" % i, [128, 512], F32, so) for i in range(4)]
                byT, bxr, bpsq = [Buf(), Buf()], [Buf(), Buf(), Buf()], [Buf() for _ in range(4)]
                n = 0
                for c in range(nchk):
                    y = yT[c % 2]
                    S.dma("sp", y[:], YTa[c].rearrange("(f p) t -> p f t", p=128), reads=[DB["YTa"]], writes=[byT[c % 2]], qbuf=byT[c % 2])
                    for t in range(4):
                        tile = 4 * c + t
                        x3 = tile % 3
                        S.dma("sp", xr[x3][:], xsrc[tile * 128:(tile + 1) * 128, :], reads=[xbuf], writes=[bxr[x3]], qbuf=bxr[x3])
                        for hf in range(2):
                            k = n % 4
                            n += 1
                            for fc in range(8):
                                S.op("pe", lambda e: e.matmul(psq[k][:, :], lhsT=y[:, fc, t * 128:(t + 1) * 128], rhs=wo[:, fc, hf * 512:(hf + 1) * 512],
                                                              start=(fc == 0), stop=(fc == 7)), reads=[byT[c % 2], bwo], writes=[bpsq[k]])
                            S.op("dve", lambda e: e.tensor_tensor(xr[x3][:, hf * 512:(hf + 1) * 512], xr[x3][:, hf * 512:(hf + 1) * 512], psq[k][:, :], ALU.add),
                                 reads=[bpsq[k], bxr[x3]], writes=[bxr[x3]])
                        S.dma("pool", dst[tile * 128:(tile + 1) * 128, :], xr[x3][:], reads=[bxr[x3]], writes=[DB[dname]], qbuf=bxr[x3])
                S.barrier()

        P.proj_phase = proj_phase
        xin_buf = Buf("xin")
        if mode == "fused":
            proj_phase(0, I["xb"], xin_buf)
            if STAGE > 1:
                compress_phase(0)
                attn_phase(0)
            if STAGE > 3:
                outproj_phase(0, I["xb"], xin_buf, X1, "X1")
                proj_phase(1, X1, DB["X1"])
                compress_phase(1)
                attn_phase(1)
                outproj_phase(1, X1, DB["X1"], out, "out")
        elif mode == 0:
            proj_phase(0, I["xb"], xin_buf)
            compress_phase(0)
            attn_phase(0)
        elif mode == 1:
            outproj_phase(0, I["xb"], xin_buf, X1, "X1", gather=False)
            proj_phase(1, X1, DB["X1"])
            compress_phase(1)
            attn_phase(1)
        else:
            outproj_phase(1, X1, DB["X1"], out, "out", gather=False, nchk=NCH // 4)
        S.barrier()
    print("instructions:", S.ninst, {k: v for k, v in S.cnt.items() if not k.startswith("q")}, "queues", S.nq)
    return P


_PROGS = {}


def _prog(mode):
    if mode not in _PROGS:
        _PROGS[mode] = build_program(mode)
    return _PROGS[mode]


def _run(mode, maps):
    P = _prog(mode)
    ms = [{k: v for k, v in m.items() if k in P.I} for m in maps]
    return run_bass_kernel_spmd(P.nc, ms, core_ids=list(range(8)))


def kernel(**inputs):
    maps = _host_inputs(inputs)
    if not os.environ.get("KMULTI"):
        res = _run("fused", maps)
        o = np.stack([np.asarray(res.results[0]["out"]), np.asarray(res.results[4]["out"])], axis=0)
        return o.astype(np.float32)
    r0 = _run(0, maps)
    for b in range(2):
        yta = np.concatenate([np.asarray(r0.results[4 * b + s]["YTo"]) for s in range(4)], axis=1)
        for s in range(4):
            maps[4 * b + s]["YTa"] = yta
    del r0
    r1 = _run(1, maps)
    qt = SEQ // 4
    for b in range(2):
        yta = np.concatenate([np.asarray(r1.results[4 * b + s]["YTo"]) for s in range(4)], axis=1)
        x1 = np.asarray(r1.results[4 * b]["X1"])
        for s in range(4):
            maps[4 * b + s]["YTa"] = np.ascontiguousarray(yta[4 * s:4 * s + 4])
            maps[4 * b + s]["X1"] = np.ascontiguousarray(x1[s * qt:(s + 1) * qt])
    del r1
    r2 = _run(2, maps)
    o = np.stack([np.concatenate([np.asarray(r2.results[4 * b + s]["out"]) for s in range(4)], axis=0) for b in range(2)], axis=0)
    return o.astype(np.float32)
```
